# Optimizing a Trainium2 kernel written in Bass

```python
import functools
import jax, jax.numpy as jnp
from jax import lax
import numpy as np

D_MODEL = 1024
BATCH = 8
SEQ = 4096
DEPTH = 1
DEC_BATCH = 128
DEC_SEQ = 1
PAST_LEN = 8192
PAGE_SIZE = 128

MIX_WIDTH = D_MODEL
POOL_WIDTH = MIX_WIDTH // 2
POOL_WINDOWS = (2, 4, 8, 16)
N_POOL_GROUPS = len(POOL_WINDOWS)
POOL_GC = POOL_WIDTH // N_POOL_GROUPS
POOL_STATE = max(POOL_WINDOWS) - 1
NSA_WIDTH = MIX_WIDTH - POOL_WIDTH
HEAD_DIM = 64
N_HEADS = NSA_WIDTH // HEAD_DIM
KV_HEADS = 2
GROUP = N_HEADS // KV_HEADS
N_BRANCH = 3
CMP_LEN = 32
CMP_STRIDE = 16
SEL_BLOCK = 64
N_SELECT = 16
WINDOW = 512
QBLK = 128
N_IN = POOL_WIDTH + NSA_WIDTH + N_BRANCH * 2 * KV_HEADS * HEAD_DIM + N_HEADS * N_BRANCH
MEM_LEN = 256
X_HEADS = 4
X_HEAD_DIM = D_MODEL // X_HEADS
X_WIDTH = X_HEADS * X_HEAD_DIM
D_FF = 2816
CONV_W = 3
EPS = 1e-6

kernel_name = "hymba_pool_nsa_convffn_decode_step"


def rmsnorm(x, g):
    xf = x.astype(jnp.float32)
    y = xf * lax.rsqrt(jnp.mean(xf * xf, axis=-1, keepdims=True) + EPS)
    return y.astype(x.dtype) * g


def masked_softmax(s, mask):
    s = jnp.where(mask, s.astype(jnp.float32), -jnp.inf)
    m = jnp.max(s, axis=-1, keepdims=True)
    m = jnp.where(jnp.isfinite(m), m, 0.0)
    e = jnp.exp(s - m)
    return e / jnp.maximum(jnp.sum(e, axis=-1, keepdims=True), 1e-30)


def mixer_inputs(x, g_mix, w_in):
    B, T, _ = x.shape
    z = rmsnorm(x, g_mix) @ w_in
    o1 = POOL_WIDTH
    o2 = o1 + NSA_WIDTH
    o3 = o2 + N_BRANCH * 2 * KV_HEADS * HEAD_DIM
    u = z[..., :o1]
    q = z[..., o1:o2].reshape(B, T, N_HEADS, HEAD_DIM)
    kv = z[..., o2:o3].reshape(B, T, N_BRANCH, 2, KV_HEADS, HEAD_DIM)
    gates = jax.nn.sigmoid(z[..., o3:]).reshape(B, T, N_HEADS, N_BRANCH)
    return u, q, kv, gates


def pool_mix(prefix, u, pos0, w_pool, pool_scale):
    B, T, C = u.shape
    P = prefix.shape[1]
    ext = jnp.concatenate([prefix, u], axis=1)
    cs = jnp.pad(jnp.cumsum(ext.astype(jnp.float32), axis=1), ((0, 0), (1, 0), (0, 0)))
    pos = pos0 + jnp.arange(T, dtype=jnp.int32)
    upto = cs[:, P + 1:P + 1 + T]
    means = []
    for gi, w in enumerate(POOL_WINDOWS):
        sl = slice(gi * POOL_GC, (gi + 1) * POOL_GC)
        win_sum = upto[..., sl] - cs[:, P + 1 - w:P + 1 - w + T, sl]
        cnt = jnp.minimum(pos + 1, w).astype(jnp.float32)[None, :, None]
        means.append(win_sum / cnt)
    d = (jnp.concatenate(means, axis=-1) - u.astype(jnp.float32)).astype(u.dtype)
    d = d.reshape(B, T, N_POOL_GROUPS, POOL_GC)
    y = jnp.einsum('btgc,gcd->btgd', d, w_pool).reshape(B, T, C) * pool_scale
    return y, ext[:, -POOL_STATE:]


def compress_blocks(rows, w_cmp, pe_cmp):
    B, T = rows.shape[:2]
    r = CMP_LEN // CMP_STRIDE
    n_chunk = T // CMP_STRIDE
    n_cmp = n_chunk - r + 1
    ch = rows[:, :n_chunk * CMP_STRIDE].reshape(B, n_chunk, CMP_STRIDE, 2, KV_HEADS, HEAD_DIM)
    w = w_cmp.reshape(2, r, CMP_STRIDE, HEAD_DIM, HEAD_DIM)
    pe = pe_cmp.reshape(2, r, CMP_STRIDE, HEAD_DIM)
    out = jnp.einsum('crsd,crsde->ce', pe, w).astype(rows.dtype)[None, None, :, None, :]
    for i in range(r):
        out = out + jnp.einsum('bnscgd,csde->bncge', ch[:, i:i + n_cmp], w[:, i])
    return out


def cmp_end_positions(n_cmp):
    return jnp.arange(n_cmp, dtype=jnp.int32) * CMP_STRIDE + (CMP_LEN - 1)


def cmp_to_sel(n_cmp, n_slc):
    cs = jnp.arange(n_cmp, dtype=jnp.int32) * CMP_STRIDE
    ss = jnp.arange(n_slc, dtype=jnp.int32) * SEL_BLOCK
    ov = (cs[:, None] + CMP_LEN - 1 >= ss[None, :]) & (cs[:, None] <= ss[None, :] + SEL_BLOCK - 1)
    return ov.astype(jnp.float32)


def sel_blocks(rows, n_slc):
    B, T = rows.shape[:2]
    rows = jnp.pad(rows, ((0, 0), (0, n_slc * SEL_BLOCK - T), (0, 0), (0, 0), (0, 0)))
    return rows.reshape(B, n_slc, SEL_BLOCK, 2, KV_HEADS, HEAD_DIM)


def nsa_core(q, gates, pos_q, kvc, cmp_end, kvs, kvw, pos_w, ovl):
    B, Tq = q.shape[:2]
    dt = q.dtype
    scale = HEAD_DIM ** -0.5
    qg = q.reshape(B, Tq, KV_HEADS, GROUP, HEAD_DIM)
    s_c = jnp.einsum('btghd,bngd->bghtn', qg, kvc[:, :, 0]) * scale
    p_c = masked_softmax(s_c, cmp_end[None, :] <= pos_q[:, None])
    o_c = jnp.einsum('bghtn,bngd->btghd', p_c.astype(dt), kvc[:, :, 1])
    n_slc = ovl.shape[1]
    imp = jnp.einsum('bghtn,nj->bgtj', p_c, ovl)
    jb = jnp.arange(n_slc, dtype=jnp.int32)[None, :]
    jq = (pos_q // SEL_BLOCK)[:, None]
    forced = (jb == 0) | (jb == jq) | (jb == jq - 1)
    imp = jnp.where(forced, jnp.inf, jnp.where(jb > jq, -jnp.inf, imp))
    k_sel = min(N_SELECT, n_slc)
    _, idx = lax.top_k(imp, k_sel)
    b_ar = jnp.arange(B)[:, None, None, None]
    g_ar = jnp.arange(KV_HEADS)[None, :, None, None]
    sel = kvs[b_ar, idx, :, :, g_ar, :]
    tok = idx[..., None] * SEL_BLOCK + jnp.arange(SEL_BLOCK, dtype=jnp.int32)
    valid = (tok <= pos_q[None, None, :, None, None]).reshape(B, KV_HEADS, 1, Tq, k_sel * SEL_BLOCK)
    s_s = jnp.einsum('btghd,bgtkrd->bghtkr', qg, sel[..., 0, :]) * scale
    p_s = masked_softmax(s_s.reshape(B, KV_HEADS, GROUP, Tq, k_sel * SEL_BLOCK), valid)
    o_s = jnp.einsum('bghtkr,bgtkrd->btghd', p_s.reshape(s_s.shape).astype(dt), sel[..., 1, :])
    s_w = jnp.einsum('btghd,bsgd->bghts', qg, kvw[:, :, 0]) * scale
    dpos = pos_q[:, None] - pos_w[None, :]
    mask_w = (dpos >= 0) & (dpos <= WINDOW) & (pos_w[None, :] >= 0)
    p_w = masked_softmax(s_w, mask_w)
    o_w = jnp.einsum('bghts,bsgd->btghd', p_w.astype(dt), kvw[:, :, 1])
    g = gates.reshape(B, Tq, KV_HEADS, GROUP, N_BRANCH)
    o = g[..., 0:1] * o_c + g[..., 1:2] * o_s + g[..., 2:3] * o_w
    return o.reshape(B, Tq, NSA_WIDTH)


def nsa_prompt(q, kv, gates, *, w_cmp, pe_cmp):
    B, T = q.shape[:2]
    kvc = compress_blocks(kv[:, :, 0], w_cmp, pe_cmp)
    n_cmp = kvc.shape[1]
    cmp_end = cmp_end_positions(n_cmp)
    n_slc = -(-T // SEL_BLOCK)
    kvs = sel_blocks(kv[:, :, 1], n_slc)
    ovl = cmp_to_sel(n_cmp, n_slc)
    kvw = jnp.pad(kv[:, :, 2], ((0, 0), (WINDOW, 0), (0, 0), (0, 0), (0, 0)))

    def one_block(bi):
        t0 = bi * QBLK
        qb = lax.dynamic_slice_in_dim(q, t0, QBLK, axis=1)
        gb = lax.dynamic_slice_in_dim(gates, t0, QBLK, axis=1)
        kwb = lax.dynamic_slice_in_dim(kvw, t0, WINDOW + QBLK, axis=1)
        pos_q = t0 + jnp.arange(QBLK, dtype=jnp.int32)
        pos_w = t0 - WINDOW + jnp.arange(WINDOW + QBLK, dtype=jnp.int32)
        return nsa_core(qb, gb, pos_q, kvc, cmp_end, kvs, kwb, pos_w, ovl)

    o = lax.map(one_block, jnp.arange(T // QBLK, dtype=jnp.int32))
    o = jnp.swapaxes(o, 0, 1).reshape(B, T, NSA_WIDTH)
    kv_rows = kv[:, :, :2].reshape(B, T, 4, KV_HEADS, HEAD_DIM)
    win_rows = kv[:, T - min(WINDOW, T):, 2]
    return o, (kv_rows, win_rows)


def nsa_sample(q, kv, gates, *, cache_kv_l, page_table, buf, w_cmp, pe_cmp):
    B, Tn = q.shape[:2]
    past = cache_kv_l[page_table]
    past = past.reshape(B, past.shape[1] * past.shape[2], 4, KV_HEADS, HEAD_DIM)
    past_len = past.shape[1]
    new4 = kv[:, :, :2].reshape(B, Tn, 4, KV_HEADS, HEAD_DIM)
    full = jnp.concatenate([past, new4], axis=1)
    T = full.shape[1]
    kvc = compress_blocks(full[:, :, 0:2], w_cmp, pe_cmp)
    n_cmp = kvc.shape[1]
    n_slc = -(-T // SEL_BLOCK)
    kvs = sel_blocks(full[:, :, 2:4], n_slc)
    kvw = jnp.concatenate([buf, kv[:, :, 2]], axis=1)
    pos_w = (past_len - buf.shape[1]) + jnp.arange(kvw.shape[1], dtype=jnp.int32)
    pos_q = past_len + jnp.arange(Tn, dtype=jnp.int32)
    o = nsa_core(q, gates, pos_q, kvc, cmp_end_positions(n_cmp), kvs, kvw, pos_w, cmp_to_sel(n_cmp, n_slc))
    new_buf = kvw[:, kvw.shape[1] - min(WINDOW, kvw.shape[1]):]
    return o, (new4, new_buf)


def mem_project(mem, g_mem, w_xkv):
    B, M, _ = mem.shape
    return (rmsnorm(mem, g_mem) @ w_xkv).reshape(B, M, 2, X_HEADS, X_HEAD_DIM)


def cross_attn(xn, mem_kv, w_xq, w_xo):
    B, T, _ = xn.shape
    q = (xn @ w_xq).reshape(B, T, X_HEADS, X_HEAD_DIM)
    s = jnp.einsum('bthd,bmhd->bhtm', q, mem_kv[:, :, 0]) * (X_HEAD_DIM ** -0.5)
    p = jax.nn.softmax(s.astype(jnp.float32), axis=-1).astype(xn.dtype)
    o = jnp.einsum('bhtm,bmhd->bthd', p, mem_kv[:, :, 1]).reshape(B, T, X_WIDTH)
    return o @ w_xo


def conv_ffn(xn, prefix, w_up, conv_w, conv_b, w_down):
    T = xn.shape[1]
    up = xn @ w_up
    ext = jnp.concatenate([prefix, up], axis=1)
    c = conv_b
    for k in range(CONV_W):
        c = c + conv_w[k] * ext[:, k:k + T]
    gate, val = jnp.split(c, 2, axis=-1)
    return (jax.nn.silu(gate) * val) @ w_down, ext[:, -(CONV_W - 1):]


def decoder_layer(h, pos0, pool_prefix, ffn_prefix, mem_kv, nsa_fn,
                  g_mix, w_in, w_pool, pool_scale, w_out, g_xattn, w_xq, w_xo,
                  g_ffn, w_up, conv_w, conv_b, w_down):
    u, q, kv, gates = mixer_inputs(h, g_mix, w_in)
    y_pool, pool_state = pool_mix(pool_prefix, u, pos0, w_pool, pool_scale)
    y_nsa, nsa_state = nsa_fn(q, kv, gates)
    h = h + jnp.concatenate([y_pool, y_nsa], axis=-1) @ w_out
    h = h + cross_attn(rmsnorm(h, g_xattn), mem_kv, w_xq, w_xo)
    y_ffn, ffn_state = conv_ffn(rmsnorm(h, g_ffn), ffn_prefix, w_up, conv_w, conv_b, w_down)
    return h + y_ffn, pool_state, nsa_state, ffn_state


def setup_inputs(seed: int = 0) -> dict:
    key = jax.random.key(seed)
    k = jax.random.split(key, 28)
    f32 = jnp.float32

    def nrm(kk, shape, scale):
        return jax.random.normal(kk, shape, f32) * scale

    def gain(kk, shape):
        return 1.0 + 0.05 * jax.random.normal(kk, shape, f32)

    n_pages = PAST_LEN // PAGE_SIZE
    n_used = DEC_BATCH * n_pages
    n_phys = n_used + n_used // 4
    page_table = jax.random.permutation(k[4], n_phys)[:n_used].reshape(DEC_BATCH, n_pages).astype(jnp.int32)
    wbuf = min(WINDOW, PAST_LEN)
    return {
        "x_prompt": nrm(k[0], (BATCH, SEQ, D_MODEL), 1.0),
        "x_sample": nrm(k[1], (DEC_BATCH, DEC_SEQ, D_MODEL), 1.0),
        "mem_prompt": nrm(k[2], (BATCH, MEM_LEN, D_MODEL), 1.0),
        "cache_kv": nrm(k[3], (DEPTH, n_phys, PAGE_SIZE, 4, KV_HEADS, HEAD_DIM), 1.0),
        "page_table": page_table,
        "cache_win": nrm(k[5], (DEPTH, DEC_BATCH, wbuf, 2, KV_HEADS, HEAD_DIM), 1.0),
        "state_pool": nrm(k[6], (DEPTH, DEC_BATCH, POOL_STATE, POOL_WIDTH), 1.0),
        "state_ffn": nrm(k[7], (DEPTH, DEC_BATCH, CONV_W - 1, 2 * D_FF), 1.0),
        "cache_mem": nrm(k[8], (DEPTH, DEC_BATCH, MEM_LEN, 2, X_HEADS, X_HEAD_DIM), 1.0),
        "g_mix": gain(k[9], (DEPTH, D_MODEL)),
        "w_in": nrm(k[10], (DEPTH, D_MODEL, N_IN), D_MODEL ** -0.5),
        "w_pool": nrm(k[11], (DEPTH, N_POOL_GROUPS, POOL_GC, POOL_GC), POOL_GC ** -0.5),
        "pool_scale": gain(k[12], (DEPTH, POOL_WIDTH)),
        "w_cmp": nrm(k[13], (DEPTH, 2, CMP_LEN, HEAD_DIM, HEAD_DIM), (CMP_LEN * HEAD_DIM) ** -0.5),
        "pe_cmp": nrm(k[14], (DEPTH, 2, CMP_LEN, HEAD_DIM), 0.1),
        "w_out": nrm(k[15], (DEPTH, MIX_WIDTH, D_MODEL), MIX_WIDTH ** -0.5),
        "g_xattn": gain(k[16], (DEPTH, D_MODEL)),
        "g_mem": gain(k[17], (DEPTH, D_MODEL)),
        "w_xq": nrm(k[18], (DEPTH, D_MODEL, X_WIDTH), D_MODEL ** -0.5),
        "w_xkv": nrm(k[19], (DEPTH, D_MODEL, 2 * X_WIDTH), D_MODEL ** -0.5),
        "w_xo": nrm(k[20], (DEPTH, X_WIDTH, D_MODEL), X_WIDTH ** -0.5),
        "g_ffn": gain(k[21], (DEPTH, D_MODEL)),
        "w_up": nrm(k[22], (DEPTH, D_MODEL, 2 * D_FF), D_MODEL ** -0.5),
        "conv_w": nrm(k[23], (DEPTH, CONV_W, 2 * D_FF), CONV_W ** -0.5),
        "conv_b": nrm(k[24], (DEPTH, 2 * D_FF), 0.01),
        "w_down": nrm(k[25], (DEPTH, D_FF, D_MODEL), D_FF ** -0.5),
        "g_final": gain(k[26], (D_MODEL,)),
    }


def reference(x_prompt, x_sample, mem_prompt, cache_kv, page_table, cache_win, state_pool, state_ffn, cache_mem,
              g_mix, w_in, w_pool, pool_scale, w_cmp, pe_cmp, w_out, g_xattn, g_mem, w_xq, w_xkv, w_xo,
              g_ffn, w_up, conv_w, conv_b, w_down, g_final):
    hp = x_prompt
    hs = x_sample
    bp = x_prompt.shape[0]
    kv_p_l, kv_s_l, win_p_l, win_s_l = [], [], [], []
    pool_p_l, pool_s_l, ffn_p_l, ffn_s_l, mem_p_l = [], [], [], [], []
    for l in range(DEPTH):
        wl = (g_mix[l], w_in[l], w_pool[l], pool_scale[l], w_out[l], g_xattn[l], w_xq[l], w_xo[l],
              g_ffn[l], w_up[l], conv_w[l], conv_b[l], w_down[l])
        mem_kv_p = mem_project(mem_prompt, g_mem[l], w_xkv[l])
        pool0 = jnp.zeros((bp, POOL_STATE, POOL_WIDTH), hp.dtype)
        ffn0 = jnp.zeros((bp, CONV_W - 1, 2 * D_FF), hp.dtype)
        nsa_p = functools.partial(nsa_prompt, w_cmp=w_cmp[l], pe_cmp=pe_cmp[l])
        hp, pool_p, (kv_p, win_p), ffn_p = decoder_layer(hp, 0, pool0, ffn0, mem_kv_p, nsa_p, *wl)
        nsa_s = functools.partial(nsa_sample, cache_kv_l=cache_kv[l], page_table=page_table,
                                  buf=cache_win[l], w_cmp=w_cmp[l], pe_cmp=pe_cmp[l])
        hs, pool_s, (kv_s, win_s), ffn_s = decoder_layer(hs, PAST_LEN, state_pool[l], state_ffn[l],
                                                         cache_mem[l], nsa_s, *wl)
        kv_p_l.append(kv_p); kv_s_l.append(kv_s); win_p_l.append(win_p); win_s_l.append(win_s)
        pool_p_l.append(pool_p); pool_s_l.append(pool_s); ffn_p_l.append(ffn_p); ffn_s_l.append(ffn_s)
        mem_p_l.append(mem_kv_p)
    y_prompt = rmsnorm(hp, g_final)
    y_sample = rmsnorm(hs, g_final)
    kv_prompt = jnp.stack(kv_p_l, 0)
    kv_sample = jnp.stack(kv_s_l, 0)
    win_prompt = jnp.stack(win_p_l, 0)
    win_sample = jnp.stack(win_s_l, 0)
    pool_prompt = jnp.stack(pool_p_l, 0)
    pool_sample = jnp.stack(pool_s_l, 0)
    ffn_prompt = jnp.stack(ffn_p_l, 0)
    ffn_sample = jnp.stack(ffn_s_l, 0)
    mem_prompt_kv = jnp.stack(mem_p_l, 0)
    return (y_prompt, y_sample, kv_prompt, kv_sample, win_prompt, win_sample,
            pool_prompt, pool_sample, ffn_prompt, ffn_sample, mem_prompt_kv)
```

```python
import os
import numpy as np
import ml_dtypes
DBG = os.environ.get('DBGSKIP', '')
DBG_OUT = bool(os.environ.get('DBGOUT', ''))
SSTOP = float(os.environ.get('SSTOP', '99'))
QUIESCE = bool(os.environ.get('QUIESCE', ''))
from contextlib import ExitStack
import concourse.bass as bass
import concourse.mybir as mybir
from concourse.bass_utils import run_bass_kernel_spmd

F32 = mybir.dt.float32
BF16 = mybir.dt.bfloat16
I32 = mybir.dt.int32
ALU = mybir.AluOpType
AF = mybir.ActivationFunctionType
AX = mybir.AxisListType

D = 1024
KD = 8
TT = 512
DFF = 2816
NFC = 22
EPS = 1e-6
NEG = 30000.0
SCALE = 64 ** -0.5
SCALE_X = 256 ** -0.5
NSLOT = 6


class Rec:
    ENG = ("pe", "act", "dve", "pool", "sp")

    def __init__(self, nc, stack, n_dma_sems=48):
        self.nc = nc
        self.eobj = {"pe": nc.tensor, "act": nc.scalar, "dve": nc.vector, "pool": nc.gpsimd, "sp": nc.sync}
        self.sem = {}
        for e in self.ENG:
            self.sem[e] = stack.enter_context(nc.semaphore("s_" + e))
        self.nd = n_dma_sems
        for i in range(n_dma_sems):
            self.sem[("d", i)] = stack.enter_context(nc.semaphore("s_d%d" % i))
        self.dval = [0] * n_dma_sems
        self.drr = 0
        self.lists = {e: [] for e in self.ENG}
        self.cnt = {e: 0 for e in self.ENG}
        self.seen = {e: {} for e in self.ENG}
        self.lastw = {}
        self.readers = {}
        self.nops = 0

    def _deps(self, e, reads, writes, extra=()):
        toks = {}
        def add(t):
            if t is None:
                return
            k, v = t
            if toks.get(k, 0) < v:
                toks[k] = v
        for r in reads:
            add(self.lastw.get(r))
            if isinstance(r, tuple) and r[0] == "P":
                for t in self.readers.get(r, ()):
                    if t[0] != e:
                        add(t)
        for w in writes:
            add(self.lastw.get(w))
            for t in self.readers.get(w, ()):
                add(t)
        for t in extra:
            add(t)
        waits = []
        for k, v in toks.items():
            if k == e:
                if e == "pe":
                    continue
                if v < self.cnt[e]:
                    continue
            if self.seen[e].get(k, 0) >= v:
                continue
            self.seen[e][k] = v
            waits.append((k, v))
        return waits

    def _commit(self, tok, reads, writes):
        for w in writes:
            self.lastw[w] = tok
            self.readers[w] = []
        for r in reads:
            self.readers.setdefault(r, []).append(tok)

    def op(self, e, fn, reads=(), writes=()):
        waits = self._deps(e, reads, writes)
        self.cnt[e] += 1
        tok = (e, self.cnt[e])
        self.lists[e].append((waits, fn, (e, 1)))
        self._commit(tok, reads, writes)
        self.nops += 1
        return tok

    def dma(self, e, out, in_, reads=(), writes=(), **kw):
        i = self.drr
        self.drr = (self.drr + 1) % self.nd
        key = ("d", i)
        extra = [(key, self.dval[i])] if self.dval[i] > 0 else []
        waits = self._deps(e, reads, writes, extra)
        self.dval[i] += 16
        tok = (key, self.dval[i])
        def fn(eng, out=out, in_=in_, kw=kw):
            return eng.dma_start(out=out, in_=in_, **kw)
        self.lists[e].append((waits, fn, (key, 16)))
        self._commit(tok, reads, writes)
        return tok

    def dma_custom(self, e, fn, reads=(), writes=()):
        i = self.drr
        self.drr = (self.drr + 1) % self.nd
        key = ("d", i)
        extra = [(key, self.dval[i])] if self.dval[i] > 0 else []
        waits = self._deps(e, reads, writes, extra)
        self.dval[i] += 16
        tok = (key, self.dval[i])
        self.lists[e].append((waits, fn, (key, 16)))
        self._commit(tok, reads, writes)
        return tok

    def wait_all(self, e, keys):
        toks = []
        for k in keys:
            t = self.lastw.get(k)
            if t is not None:
                toks.append(t)
        waits = self._deps(e, (), (), toks)
        self.lists[e].append((waits, None, None))

    def flush(self):
        nc = self.nc
        lists = self.lists
        self.lists = {e: [] for e in self.ENG}
        sem = self.sem
        def replay(eng, items):
            for waits, fn, inc in items:
                for k, v in waits:
                    eng.wait_ge(sem[k], v)
                if fn is None:
                    continue
                ins = fn(eng)
                ins.then_inc(sem[inc[0]], inc[1])
        with nc.Block() as block:
            @block.tensor
            def _(eng):
                replay(eng, lists["pe"])
            @block.scalar
            def _(eng):
                replay(eng, lists["act"])
            @block.vector
            def _(eng):
                replay(eng, lists["dve"])
            @block.gpsimd
            def _(eng):
                replay(eng, lists["pool"])
            @block.sync
            def _(eng):
                replay(eng, lists["sp"])


def build_program(T, NS, do_prompt=True, do_sample=True, dbg=False, stop=99, nphys=10240):
    NST = T // TT
    NQT = T // 128
    nc = bass.Bass("TRN2", target_bir_lowering=False)
    stack = ExitStack()
    R = Rec(nc, stack)

    def din(name, shape, dt=F32):
        return nc.dram_tensor(name, list(shape), dt, kind="ExternalInput").ap()

    def dout(name, shape, dt=F32):
        return nc.dram_tensor(name, list(shape), dt, kind="ExternalOutput").ap()

    pstack = ExitStack()
    stacks_to_close = [pstack]
    PERSIST = {"WR", "IDENT", "BDK", "BDV", "PET", "BIAS", "WPOOL", "GT", "GFIN", "PSCT", "CW", "CB", "SM", "SSQ", "RSTD", "M8"}

    def sb(name, shape, dt):
        st_ = stack if name in PERSIST else pstack
        return st_.enter_context(nc.sbuf_tensor(name, list(shape), dt))

    xp = din("xp", [T, D])
    memp = din("memp", [256, D])
    wall = din("wall", [61, 128, 8, 256])
    wxkv_r = [wall[i] for i in range(0, 8)]
    win_r = [wall[i] for i in range(8, 16)]
    wout_r = [wall[i] for i in range(16, 20)]
    wxq_r = [wall[i] for i in range(20, 24)]
    wxo_r = [wall[i] for i in range(24, 28)]
    wup_r = [wall[i] for i in range(28, 50)]
    wdn_r = [wall[i] for i in range(50, 61)]
    wpool_r = din("wpool_r", [128, 4, 128])
    bdk_r = din("bdk_r", [128, 32, 128])
    bdv_r = din("bdv_r", [128, 32, 128])
    peT_r = din("peT_r", [128, 2, 32])
    gT_r = din("gT_r", [128, 4, 8])
    gfin_r = din("gfin_r", [1, D])
    pscT_r = din("pscT_r", [128, 4])
    cwT_r = din("cwT_r", [128, NFC, 2, 3])
    cbT_r = din("cbT_r", [128, NFC, 2])
    c_ident = din("c_ident", [128, 128])
    c_tri = din("c_tri", [128, 2, 128])
    c_cmask = din("c_cmask", [128, 16, 128])
    c_fbrel = din("c_fbrel", [128, 130])
    c_ovl = din("c_ovl", [128, 2, 64])
    c_expand = din("c_expand", [128, T])
    c_invc = din("c_invc", [128, 4, 16])

    y_p = dout("y_p", [T, D])
    kv_p = dout("kv_p", [T, 512])
    win_p = dout("win_p", [512, 256])
    pool_p = dout("pool_p", [15, 512])
    ffn_p = dout("ffn_p", [2, 2 * DFF])
    memkv_p = dout("memkv_p", [256, 2048])
    if DBG_OUT:
        dbg_h1 = dout("dbg_h1", [T, D])
        dbg_h2 = dout("dbg_h2", [T, D])
        dbg_mix = dout("dbg_mix", [8, 128, T], BF16)

    NPHYS = nphys
    xs_d = din("xs", [16, D])
    pool_kv = din("pool_kv", [NPHYS * 128, 512])
    ptab = din("ptab", [1, 16 * 64], I32)
    cwin = din("cwin", [16, 512, 256])
    spool = din("spool", [16, 15, 512])
    sffn = din("sffn", [16, 2, 2 * DFF])
    cmem = din("cmem", [16, 256, 2048])
    wouth_r = din("wouth_r", [128, 4, 1024])
    c_identf = din("c_identf", [128, 128])
    c_ovls = din("c_ovls", [128, 4, 128])
    c_cval = din("c_cval", [128, 4])
    c_fbs = din("c_fbs", [2, 129])
    c_sel = din("c_sel", [16, 16, 128])
    y_s = dout("y_s", [16, D])
    kv_s = dout("kv_s", [16, 512])
    win_s = dout("win_s", [16, 512, 256])
    pool_s = dout("pool_s", [16, 15, 512])
    ffn_s = dout("ffn_s", [16, 2, 2 * DFF])
    scr_sel = nc.dram_tensor("scr_sel", [16, 2, 128], F32, kind="Internal").ap()
    if DBG_OUT:
        dbg_s = dout("dbg_s", [2, 16, D])
        dbg_mx = dout("dbg_mx", [128, 12, 16], BF16)

    PS = [stack.enter_context(nc.psum_tensor("ps%d" % i, [128, 512], F32)) for i in range(8)]

    def PSB(i):
        return PS[i][:].bitcast(BF16)

    WR = sb("WR", [128, NSLOT, 2048], BF16)
    BDK = sb("BDK", [128, 32, 128], BF16)
    BDV = sb("BDV", [128, 32, 128], BF16)
    PET = sb("PET", [128, 2, 32], BF16)
    BIAS = sb("BIAS", [128, 2], F32)
    SM = sb("SM", [128, 64], F32)
    M8 = sb("M8", [128, 16], F32)
    IDENT = sb("IDENT", [128, 128], BF16)
    WPOOL = sb("WPOOL", [128, 4, 128], BF16)
    GT = sb("GT", [128, 4, 8], F32)
    GFIN = sb("GFIN", [128, D], F32)
    PSCT = sb("PSCT", [128, 4], F32)
    CW = sb("CW", [128, NFC, 2, 3], F32)
    CB = sb("CB", [128, NFC, 2], F32)
    SSQ = sb("SSQ", [128, 8], F32)
    RSTD = sb("RSTD", [128, 8], F32)
    KE = [sb("KE0", [128, T], BF16), sb("KE1", [128, T], BF16)]
    KW = sb("KW", [128, T], BF16)
    VS1 = sb("VS1", [128, NQT, 2, 65], BF16)
    VW1 = sb("VW1", [128, NQT, 2, 65], BF16)
    KC = sb("KC", [128, 256], BF16)
    VCT = sb("VCT", [128, 256], BF16)
    VC1 = sb("VC1", [128, 2, 2, 129], BF16)
    X = sb("X", [128, 4, D], F32)
    BIG = sb("BIG", [128, NFC * 512], BF16)
    FSCR = sb("FSCR", [128, 5232], F32)
    UHALO = sb("UHALO", [128, 4, 16], F32)
    XNT = sb("XNT", [128, 8, 512], BF16)
    MIXT = sb("MIXT", [128, 8, 512], BF16)
    QM = sb("QM", [128, 8, 512], BF16)
    XKC = sb("XKC", [128, 528], BF16)
    XVC = sb("XVC", [128, 528], BF16)
    GATES = sb("GATES", [128, 4, 24], F32)
    ET = sb("ET", [128, 4, 512], BF16)
    OACC = sb("OACC", [128, 8, 64], F32)
    OB = sb("OB", [128, 512], BF16)
    OTMP = sb("OTMP", [128, 4, 64], F32)
    IMP = sb("IMP", [128, 64], F32)
    IMPF = sb("IMPF", [128, 64], F32)
    IMPW = sb("IMPW", [128, 64], F32)
    MSEL = sb("MSEL", [128, 128], BF16)
    TRI = sb("TRI", [128, 2, 128], BF16)
    CMASK = sb("CMASK", [128, 16, 128], BF16)
    FBREL = sb("FBREL", [128, 130], F32)
    INVC = sb("INVC", [128, 4, 16], F32)
    CARRY = sb("CARRY", [128, NFC, 2, 2], F32)
    KXT = sb("KXT", [128, 8, 256], BF16)
    VX1 = sb("VX1", [128, 2, 4, 257], BF16)
    UP = FSCR[:, 0:2056].rearrange("p (a b c) -> p a b c", a=2, b=2)
    CGV = FSCR[:, 2056:4104].rearrange("p (a b c) -> p a b c", a=2, b=2)
    SG = FSCR[:, 4104:5128].rearrange("p (a c) -> p a c", a=2)
    TOK = sb("TOK", [128, 4, 512], F32)
    FTOK = sb("FTOK", [2, 2, 256], F32)

    ACTT = BIG[:, 0:NFC * 512].rearrange("p (c t) -> p c t", t=512)
    XS = BIG[:, 0:4 * D].rearrange("p (j d) -> p j d", d=D)
    OX = BIG[:, 4 * D:8 * D].rearrange("p (j d) -> p j d", d=D)
    UT = FSCR[:, 0:2112].rearrange("p (g t) -> p g t", t=528)
    TA = FSCR[:, 2112:2640]
    TB = FSCR[:, 2640:3168]
    DT = BIG[:, 4096:6144].rearrange("p (g t) -> p g t", t=512)
    WTOK = FSCR[:, 3168:4192].rearrange("p (j c) -> p j c", c=256)
    UTOK = FSCR[:, 4192:4704]
    PXT = ET[:, :, :].rearrange("p (a b) t -> p a b t", a=2)
    FFN_KEYS = ["WTOK", "UTOK"] + [("UP", a_, b_) for a_ in range(2) for b_ in range(2)] + [("CGV", a_, b_) for a_ in range(2) for b_ in range(2)] + [("SG", 0), ("SG", 1)]

    wq = {"n": 0}

    def wload(src_ap):
        s = wq["n"] % NSLOT
        wq["n"] += 1
        nel = 1
        for d_ in src_ap.shape[1:]:
            nel *= d_
        dst = WR[:, s, 0:nel].rearrange("p (a b) -> p a b", b=256)
        thr = int(os.environ.get('WTHR', '2'))
        hist = wq.setdefault("hist", [])
        if len(hist) >= thr:
            R.wait_all("pool", [hist[-thr]])
        R.dma("pool", dst, src_ap, writes=[("W", s), ("Wd", wq["n"])], max_dma_last_dim=256)
        hist.append(("Wd", wq["n"]))
        return s

    def mm(out, lhsT, rhs, start=True, stop=True):
        return lambda pe: pe.matmul(out, lhsT, rhs, start=start, stop=stop)

    def pe_group(mms, reads, writes):
        def fn(pe, mms=mms):
            ins = None
            for (o, l, r, st_, sp_) in mms:
                ins = pe.matmul(o, l, r, start=st_, stop=sp_)
            return ins
        return R.op("pe", fn, reads, writes)

    def pe_transposes(items, reads, writes):
        def fn(pe, items=items):
            ins = None
            for (o, i_) in items:
                ins = pe.transpose(o, i_, IDENT[:, :])
            return ins
        return R.op("pe", fn, list(reads) + ["IDENT"], writes)

    cp_rr = {"n": 0}

    def copy_any(out, in_, reads, writes, scale=None, engines=("act", "dve")):
        e = engines[cp_rr["n"] % len(engines)]
        cp_rr["n"] += 1
        rd = list(reads)
        if e == "act":
            if scale is None:
                R.op("act", lambda a: a.activation(out, in_, AF.Copy), rd, writes)
            else:
                R.op("act", lambda a: a.activation(out, in_, AF.Copy, scale=scale), rd, writes)
        else:
            eng = e
            if scale is None:
                R.op(eng, lambda v: v.tensor_copy(out, in_), rd, writes)
            else:
                R.op(eng, lambda v: v.tensor_scalar(out, in_, scale, None, ALU.mult), rd, writes)

    out_keys = []

    def finish():
        for e_ in ("pe", "act", "dve", "pool"):
            for _ in range(int(os.environ.get('PADNOP', '0'))):
                R.op(e_, (lambda eng: eng.nop()), (), ())
        R.wait_all("sp", out_keys + [("W", s_) for s_ in range(NSLOT)])
        fin = [(e_, R.cnt[e_]) for e_ in ("pe", "act", "dve", "pool") if R.cnt[e_] > 0]
        R.lists["sp"].append((fin, None, None))
        print('ops recorded', R.nops, R.cnt)
        R.flush()
        for st_ in stacks_to_close:
            st_.close()
        stack.close()
        return nc

    def ld(e, dst, src, key):
        R.dma(e, dst, src, writes=[key])

    ld("pool", IDENT[:, :], c_ident, "IDENT")
    ld("pool", TRI[:, :, :], c_tri, "TRI")
    ld("pool", CMASK[:, :, :], c_cmask, "CMASK")
    ld("sp", FBREL[:, :], c_fbrel, "FBREL")
    ld("sp", INVC[:, :, :], c_invc, "INVC")
    ld("pool", WPOOL[:, :, :], wpool_r, "WPOOL")
    ld("pool", BDK[:, :, :], bdk_r, "BDK")
    ld("pool", BDV[:, :, :], bdv_r, "BDV")
    ld("pool", PET[:, :, :], peT_r, "PET")
    ld("sp", GT[:, :, :], gT_r, "GT")
    ld("sp", GFIN[:, :], gfin_r[0:1, :].broadcast_to([128, D]), "GFIN")
    ld("sp", PSCT[:, :], pscT_r, "PSCT")
    ld("sp", CW[:, :, :, :], cwT_r, "CW")
    ld("sp", CB[:, :, :], cbT_r, "CB")
    ld("pool", KE[0][64:128, :], c_expand[64:128, :], "KE0x")
    ld("pool", KE[1][0:64, :], c_expand[0:64, :], "KE1x")
    for g in range(2):
        ld("pool", VC1[:, :, g, 65:129], c_ovl, ("VC1c", g))
    if stop <= -3:
        return finish()
    R.op("pool", lambda p: p.memset(VS1[:, :, :, 64:65], 1.0), (), ["VS1o"])
    R.op("pool", lambda p: p.memset(VW1[:, :, :, 64:65], 1.0), (), ["VW1o"])
    R.op("pool", lambda p: p.memset(VC1[:, :, :, 64:65], 1.0), (), ["VC1o"])
    R.op("pool", lambda p: p.memset(VX1[:, :, :, 256:257], 1.0), (), ["VX1o"])
    R.op("pool", lambda p: p.memset(KC[:, :], 0.0), (), ["KC"])
    R.op("pool", lambda p: p.memset(VCT[:, :], 0.0), (), ["VCT"])
    R.op("pool", lambda p: p.memset(CARRY[:, :, :, :], 0.0), (), ["CARRY"])
    R.op("pool", lambda p: p.memset(XKC[:, :], 0.0), (), ["XKC"])
    R.op("pool", lambda p: p.memset(XVC[:, :], 0.0), (), ["XVC"])
    R.op("dve", lambda v: v.memset(SSQ[:, :], 0.0), (), ["SSQ"])

    for c in range(2):
        BD = BDK if c == 0 else BDV
        mms = [(PS[0][:, c:c + 1], BD[:, s, :], PET[:, c, s:s + 1], s == 0, s == 31) for s in range(32)]
        pe_group(mms, ["BDK", "BDV", "PET"], [("P", 0)])
        R.op("dve", lambda v, c=c: v.tensor_copy(BIAS[:, c:c + 1], PS[0][:, c:c + 1]), [("P", 0)], ["BIAS"])

    if stop <= -2:
        return finish()
    def norm_T(ntile, gsel, ncols_out):
        for j in range(ntile):
            R.op("act", lambda a, j=j: a.activation(XS[:, j, :], X[:, j, :], AF.Square, accum_out=SSQ[:, j:j + 1]),
                 ["X"], ["BIG", "SSQ"])
        R.op("dve", lambda v: v.tensor_scalar(RSTD[:, 0:ntile], SSQ[:, 0:ntile], 1.0 / D, EPS, ALU.mult, ALU.add),
             ["SSQ"], ["RSTD"])
        R.op("act", lambda a: a.sqrt(RSTD[:, 0:ntile], RSTD[:, 0:ntile]), ["RSTD"], ["RSTD"])
        R.op("dve", lambda v: v.reciprocal(RSTD[:, 0:ntile], RSTD[:, 0:ntile]), ["RSTD"], ["RSTD"])
        R.op("dve", lambda v: v.memset(SSQ[:, :], 0.0), ["RSTD"], ["SSQ"])
        for j in range(ntile):
            if j % 2 == 0:
                R.op("act", lambda a, j=j: a.activation(XS[:, j, :], X[:, j, :], AF.Copy, scale=RSTD[:, j:j + 1]),
                     ["X", "RSTD"], ["BIG"])
            else:
                R.op("dve", lambda v, j=j: v.tensor_scalar(XS[:, j, :], X[:, j, :], RSTD[:, j:j + 1], None, ALU.mult),
                     ["X", "RSTD"], ["BIG"])
        for k2 in range(4):
            bank = 6 + (k2 % 2)
            pv = PSB(bank)
            items = []
            for kk in range(2):
                k = 2 * k2 + kk
                for j in range(ntile):
                    items.append((pv[:, kk * 512 + j * 128: kk * 512 + (j + 1) * 128], XS[:, j, k * 128:(k + 1) * 128]))
            pe_transposes(items, ["BIG"], [("P", bank)])
            for kk in range(2):
                k = 2 * k2 + kk
                copy_any(XNT[:, k, 0:ntile * 128], pv[:, kk * 512: kk * 512 + ntile * 128],
                         [("P", bank), "GT"], ["XNT"], scale=GT[:, gsel, k:k + 1])

    wpend = []

    def stream_list_for_st(st):
        lst = []
        for c in range(8):
            lst.append(win_r[c])
        for c in range(4):
            lst.append(wout_r[c])
        for c in range(4):
            lst.append(wxq_r[c])
        for c in range(4):
            lst.append(wxo_r[c])
        for c in range(NFC):
            lst.append(wup_r[c])
        for c in range(11):
            lst.append(wdn_r[c])
        return lst

    full_stream = [wxkv_r[c] for c in range(8)]
    if 'W' in DBG:
        full_stream += [wout_r[c] for c in range(4)] * 2
    if 'R' in DBG:
        full_stream += [wout_r[c] for c in range(4)] + [wxq_r[c] for c in range(4)]
    if 'D' in DBG:
        full_stream += [wdn_r[c] for c in range(8)]
    if 'U' in DBG:
        full_stream += [wup_r[c] for c in range(8)]
    if 'Q' in DBG:
        full_stream += [wxq_r[c] for c in range(4)] * 2
    for st in range(NST):
        full_stream += stream_list_for_st(st)
    if do_sample:
        full_stream += stream_list_for_st(0)
    sp_ = {"next": 0}

    def prefetch(upto):
        while sp_["next"] < min(upto, len(full_stream)):
            s = wload(full_stream[sp_["next"]])
            wpend.append(s)
            sp_["next"] += 1

    cons = {"n": 0}

    def take(n):
        prefetch(cons["n"] + n)
        slots = [(cons["n"] + i) % NSLOT for i in range(n)]
        cons["n"] += n
        return slots

    def release():
        prefetch(cons["n"] + NSLOT - 0)

    R.dma("sp", X[:, 0:2, :], memp.rearrange("(j p) d -> p j d", p=128), writes=["X"])
    norm_T(2, 2, 256)
    if stop <= -1:
        return finish()
    for c in range(8):
        (s,) = take(1)
        W = WR[:, s, :].rearrange("p (k n) -> p k n", n=256)
        if stop <= -0.7:
            R.op("dve", lambda v, s=s: v.tensor_copy(SM[:, 0:8], WR[:, s, 0:8]), [("W", s)], ["SM"])
            release()
            continue
        for mt in range(2):
            bank = mt
            mms = [(PS[bank][:, 0:256], XNT[:, k, mt * 128:(mt + 1) * 128], W[:, k, :], k == 0, k == 7) for k in range(8)]
            pe_group(mms, ["XNT", ("W", s)], [("P", bank)])
            if 'a' not in DBG:
                R.op("act", lambda a, bank=bank, mt=mt, c=c: a.activation(TOK[:, (c % 2) * 2 + mt, 0:256], PS[bank][:, 0:256], AF.Copy),
                     [("P", bank)], [("TOKm", c % 2)])
            if c >= 4 and 'v' not in DBG:
                hx = c - 4
                R.op("dve", lambda v, bank=bank, mt=mt, hx=hx: v.tensor_copy(VX1[:, mt, hx, 0:256], PS[bank][:, 0:256]),
                     [("P", bank)], ["VX1"])
        if 'd' not in DBG:
          R.dma("sp", memkv_p[:, c * 256:(c + 1) * 256].rearrange("(m p) c -> p m c", p=128), TOK[:, (c % 2) * 2:(c % 2) * 2 + 2, 0:256],
              reads=[("TOKm", c % 2)], writes=[("o_memkv", c)])
          out_keys.append(("o_memkv", c))
        if c < 4 and stop > -0.5:
            hx = c
            for dc in range(2):
                bank = 2 + dc
                mms = [(PS[bank][:, 0:256], W[:, k, dc * 128:(dc + 1) * 128], XNT[:, k, 0:256], k == 0, k == 7) for k in range(8)]
                pe_group(mms, ["XNT", ("W", s)], [("P", bank)])
                copy_any(KXT[:, hx * 2 + dc, :], PS[bank][:, 0:256], [("P", bank)], ["KXT"])
        release()
    if stop <= -0.4:
        return finish()


    def st_body(st):
        last = (st == NST - 1)
        c0 = st * TT
        if 'x' not in DBG:
            R.dma("sp", X[:, :, :], xp[c0:c0 + TT, :].rearrange("(j p) d -> p j d", p=128), writes=["X"])
        if 'n' not in DBG:
            norm_T(4, 0, 512)
        if 'm' not in DBG:
            R.op("dve", lambda v: v.memset(SM[:, 40:41], 0.0), (), ["FS"] + FFN_KEYS)
        if st > 0:
            R.op("pool", lambda p: p.tensor_copy(UT[:, :, 0:16], UHALO[:, :, :]), ["UHALO"], ["FS"])
            R.op("pool", lambda p: p.tensor_copy(XKC[:, 0:16], XKC[:, 512:528]), ["XKC"], ["XKC"])
            R.op("pool", lambda p: p.tensor_copy(XVC[:, 0:16], XVC[:, 512:528]), ["XVC"], ["XVC"])
        else:
            if 'u' not in DBG:
                R.op("pool", lambda p: p.memset(UT[:, :, 0:16], 0.0), (), ["FS"])
        pb = {"n": 0}

        def nb():
            b = pb["n"] % 2
            pb["n"] += 1
            return b

        for c in range(8):
            (s,) = take(1)
            W = WR[:, s, :].rearrange("p (k n) -> p k n", n=256)
            wk = ("W", s)
            if c >= int(os.environ.get('DBGC', '99')):
                release()
                continue
            if c in (0, 1):
                for hh in range(2):
                    gi = 2 * c + hh
                    b = nb()
                    pe_group([(PS[b][:, :], W[:, k, hh * 128:(hh + 1) * 128], XNT[:, k, :], k == 0, k == 7) for k in range(8)],
                             ["XNT", wk], [("P", b)])
                    copy_any(UT[:, gi, 16:528], PS[b][:, :], [("P", b)], ["FS"])
                if last:
                    b = nb()
                    pe_group([(PS[b][:, 0:256], XNT[:, k, 384:512], W[:, k, :], k == 0, k == 7) for k in range(8)],
                             ["XNT", wk], [("P", b)])
                    copy_any(UTOK[:, c * 256:(c + 1) * 256], PS[b][:, 0:256], [("P", b)], ["UTOK"])
            elif c in (2, 3):
                for hh in range(2):
                    jq = 2 * (c - 2) + hh
                    b = nb()
                    pe_group([(PS[b][:, :], W[:, k, hh * 128:(hh + 1) * 128], XNT[:, k, :], k == 0, k == 7) for k in range(8)],
                             ["XNT", wk], [("P", b)])
                    R.op("act", lambda a, b=b, jq=jq: a.activation(QM[0:64, jq, :], PS[b][0:64, :], AF.Copy),
                         [("P", b)], [("QMq", jq)])
                    R.op("dve", lambda v, b=b, jq=jq: v.tensor_copy(QM[64:128, 4 + jq, :], PS[b][64:128, :]),
                         [("P", b)], [("QMq", 4 + jq)])
            elif c in (4, 5, 6):
                if c == 4:
                    for hh, dst, key in ((0, XKC, "XKC"), (1, XVC, "XVC")):
                        b = nb()
                        pe_group([(PS[b][:, :], W[:, k, hh * 128:(hh + 1) * 128], XNT[:, k, :], k == 0, k == 7) for k in range(8)],
                                 ["XNT", wk], [("P", b)])
                        copy_any(dst[:, 16:528], PS[b][:, :], [("P", b)], [key])
                elif c == 5:
                    b = nb()
                    pe_group([(PS[b][:, :], W[:, k, 0:128], XNT[:, k, :], k == 0, k == 7) for k in range(8)],
                             ["XNT", wk], [("P", b)])
                    R.op("act", lambda a, b=b: a.activation(KE[0][0:64, c0:c0 + TT], PS[b][0:64, :], AF.Copy),
                         [("P", b)], [("KE", 0, st)])
                    R.op("dve", lambda v, b=b: v.tensor_copy(KE[1][64:128, c0:c0 + TT], PS[b][64:128, :]),
                         [("P", b)], [("KE", 1, st)])
                else:
                    b = nb()
                    pe_group([(PS[b][:, :], W[:, k, 0:128], XNT[:, k, :], k == 0, k == 7) for k in range(8)],
                             ["XNT", wk], [("P", b)])
                    copy_any(KW[:, c0:c0 + TT], PS[b][:, :], [("P", b)], [("KW", st)])
                for j in range(4):
                    kt = 4 * st + j
                    b = nb()
                    pe_group([(PS[b][:, 0:256], XNT[:, k, j * 128:(j + 1) * 128], W[:, k, :], k == 0, k == 7) for k in range(8)],
                             ["XNT", wk], [("P", b)])
                    if c == 4:
                        R.op("act", lambda a, b=b, j=j: a.activation(TOK[:, j, 0:256], PS[b][:, 0:256], AF.Copy),
                             [("P", b)], ["TOK", ("TOKm", 0), ("TOKm", 1)])
                    elif c == 5:
                        R.op("act", lambda a, b=b, j=j: a.activation(TOK[:, j, 256:512], PS[b][:, 0:256], AF.Copy),
                             [("P", b)], ["TOK", ("TOKm", 0), ("TOKm", 1)])
                        R.op("dve", lambda v, b=b, kt=kt: v.tensor_copy(
                            VS1[:, kt, :, 0:64], PS[b][:, 128:256].rearrange("p (g d) -> p g d", d=64)),
                            [("P", b)], [("VS1", kt)])
                    else:
                        R.op("dve", lambda v, b=b, kt=kt: v.tensor_copy(
                            VW1[:, kt, :, 0:64], PS[b][:, 128:256].rearrange("p (g d) -> p g d", d=64)),
                            [("P", b)], [("VW1", kt)])
                        if last:
                            R.op("act", lambda a, b=b, j=j: a.activation(WTOK[:, j, :], PS[b][:, 0:256], AF.Copy),
                                 [("P", b)], ["WTOK"])
                if c == 5:
                    R.dma("sp", kv_p[c0:c0 + TT, :].rearrange("(j p) c -> p j c", p=128), TOK[:, :, :],
                          reads=["TOK"], writes=[("o_kv", st)])
                    out_keys.append(("o_kv", st))
                if c == 6 and last:
                    R.dma("sp", win_p.rearrange("(j p) c -> p j c", p=128), WTOK[:, :, :], reads=["WTOK"], writes=["o_win"])
                    out_keys.append("o_win")
            else:
                for j in range(4):
                    b = nb()
                    pe_group([(PS[b][:, 0:24], XNT[:, k, j * 128:(j + 1) * 128], W[:, k, 0:24], k == 0, k == 7) for k in range(8)],
                             ["XNT", wk], [("P", b)])
                    R.op("act", lambda a, b=b, j=j: a.activation(GATES[:, j, :], PS[b][:, 0:24], AF.Sigmoid),
                         [("P", b)], ["GATES"])
            release()
        if last and 'p' not in DBG:
            R.dma("sp", pool_p, UTOK[113:128, :], reads=["UTOK"], writes=["o_pool"])
            out_keys.append("o_pool")

        R.op("pool", lambda p: p.tensor_copy(UHALO[:, :, :], UT[:, :, 512:528]), ["FS"], ["UHALO"])
        if stop <= 1:
            return
        for gi, w in enumerate((2, 4, 8, 16)):
            src = UT[:, gi, :]
            cur = src
            lo = 0
            sh = 1
            tmps = [TA, TB]
            ti = 0
            while sh < w:
                dst = tmps[ti % 2]
                ti += 1
                lo2 = lo + sh
                R.op("pool", lambda p, dst=dst, cur=cur, lo2=lo2, sh=sh: p.tensor_tensor(
                    dst[:, lo2:528], cur[:, lo2:528], cur[:, lo2 - sh:528 - sh], ALU.add), ["FS"], ["FS"])
                cur = dst
                lo = lo2
                sh *= 2
            R.op("dve", lambda v, cur=cur, gi=gi, w=w: v.scalar_tensor_tensor(
                DT[:, gi, :], cur[:, 16:528], 1.0 / w, UT[:, gi, 16:528], ALU.mult, ALU.subtract), ["FS"], ["BIG"])
            if st == 0:
                R.op("dve", lambda v, cur=cur, gi=gi: v.tensor_tensor(SM[:, 0:16], cur[:, 16:32], INVC[:, gi, :], ALU.mult),
                     ["FS", "INVC"], ["SM"])
                R.op("dve", lambda v, gi=gi: v.tensor_tensor(DT[:, gi, 0:16], SM[:, 0:16], UT[:, gi, 16:32], ALU.subtract),
                     ["FS", "SM"], ["BIG"])
            b = nb()
            pe_group([(PS[b][:, :], WPOOL[:, gi, :], DT[:, gi, :], True, True)], ["BIG", "WPOOL"], [("P", b)])
            copy_any(MIXT[:, gi, :], PS[b][:, :], [("P", b), "PSCT"], ["MIXT"], scale=PSCT[:, gi:gi + 1])

        r0 = 1 if st == 0 else 0
        nbk = 32 - r0
        i0 = 32 * st - 1 + r0
        for cc_, (XC, BD, DST, key) in enumerate(((XKC, BDK, KC, "KC"), (XVC, BDV, VCT, "VCT"))):
            b = nb()
            mms = []
            for s in range(32):
                a0 = 16 * r0 + s
                mms.append((PS[b][:, 0:nbk], BD[:, s, :], XC[:, a0:a0 + 16 * (nbk - 1) + 1:16], s == 0, s == 31))
            pe_group(mms, ["XKC", "XVC", "BDK", "BDV"], [("P", b)])
            R.op("act", lambda a, b=b, DST=DST, cc_=cc_: a.activation(
                DST[:, i0:i0 + nbk], PS[b][:, 0:nbk], AF.Identity, bias=BIAS[:, cc_:cc_ + 1]), [("P", b), "BIAS"], [key])
        nts = sorted(set([i0 // 128, (i0 + nbk - 1) // 128]))
        for nt in nts:
            pv = PSB(6)
            pe_transposes([(pv[:, 0:128], VCT[:, nt * 128:(nt + 1) * 128])], ["VCT"], [("P", 6)])
            R.op("dve", lambda v, nt=nt, pv=pv: v.tensor_copy(
                VC1[:, nt, :, 0:64], pv[:, 0:128].rearrange("p (g d) -> p g d", d=64)), [("P", 6)], ["VC1"])

        if stop <= 2:
            return
        for jq in range(4):
            m = 4 * st + jq
            qc = slice(jq * 128, (jq + 1) * 128)
            sbk = {"n": 0}

            def sbank():
                b = 2 + (sbk["n"] % 2)
                sbk["n"] += 1
                return b
            etn = {"n": 0}

            def etbuf():
                b = etn["n"] % 4
                etn["n"] += 1
                return b
            G3 = GATES[:, jq, :].rearrange("p (h b) -> p h b", b=3)
            qkeys = [("QMq", h) for h in range(8)]
            for g in range(2):
                ph = slice(64 * g, 64 * g + 64)
                ncv = 8 * m + 7
                ntl = [0] if ncv <= 128 else [0, 1]
                ets = []
                for nt in ntl:
                    rows = 128 if nt == 0 else 127
                    b = sbank()
                    pe_group([(PS[b][0:rows, :], KC[ph, nt * 128:nt * 128 + rows], QM[ph, 4 * g:4 * g + 4, qc], True, True)],
                             ["KC"] + qkeys, [("P", b)])
                    eb = etbuf()
                    R.op("act", lambda a, b=b, eb=eb, rows=rows: a.activation(ET[0:rows, eb, :], PS[b][0:rows, :], AF.Exp, scale=SCALE),
                         [("P", b)], [("ET", eb)])
                    dl = m - 16 * nt
                    if dl <= 15:
                        R.op("dve", lambda v, eb=eb, rows=rows, dl=dl: v.tensor_tensor(
                            ET[0:rows, eb, :].rearrange("p (h q) -> p h q", q=128),
                            ET[0:rows, eb, :].rearrange("p (h q) -> p h q", q=128),
                            CMASK[0:rows, dl, :].unsqueeze(1).broadcast_to([rows, 4, 128]), ALU.mult),
                            [("ET", eb), "CMASK"], [("ET", eb)])
                    ets.append((nt, rows, eb))
                for half in range(2):
                    bank = 4 + half
                    mms = []
                    for hh in range(2):
                        hl = 2 * half + hh
                        for ii, (nt, rows, eb) in enumerate(ets):
                            mms.append((PS[bank][:, hh * 129:(hh + 1) * 129], ET[0:rows, eb, hl * 128:(hl + 1) * 128],
                                        VC1[0:rows, nt, g, :], ii == 0, ii == len(ets) - 1))
                    pe_group(mms, [("ET", e_[2]) for e_ in ets] + ["VC1", ("VC1c", g), "VC1o"], [("P", bank)])
                for half in range(2):
                    bank = 4 + half
                    U = PS[bank][:, 0:258].rearrange("p (h c) -> p h c", c=129)
                    R.op("dve", lambda v, U=U, half=half: v.tensor_scalar(
                        SM[:, 2 * half:2 * half + 2], U[:, :, 64], 1e-30, None, ALU.max), [("P", bank)], ["SM"])
                R.op("dve", lambda v: v.reciprocal(SM[:, 0:4], SM[:, 0:4]), ["SM"], ["SM"])
                R.op("dve", lambda v, g=g, G3=G3: v.tensor_tensor(SM[:, 4:8], SM[:, 0:4], G3[:, 4 * g:4 * g + 4, 0], ALU.mult),
                     ["SM", "GATES"], ["SM"])
                for half in range(2):
                    bank = 4 + half
                    U = PS[bank][:, 0:258].rearrange("p (h c) -> p h c", c=129)
                    R.op("dve", lambda v, U=U, half=half, g=g: v.tensor_tensor(
                        OACC[:, 4 * g + 2 * half:4 * g + 2 * half + 2, :], U[:, :, 0:64],
                        SM[:, 4 + 2 * half:6 + 2 * half].unsqueeze(2).broadcast_to([128, 2, 64]), ALU.mult),
                        [("P", bank), "SM"], ["OACC"])
                    for hh in range(2):
                        hl = 2 * half + hh
                        if hl == 0:
                            R.op("dve", lambda v, U=U, hh=hh, hl=hl: v.tensor_scalar(
                                IMP[:, :], U[:, hh, 65:129], SM[:, hl:hl + 1], None, ALU.mult), [("P", bank), "SM"], ["IMP"])
                        else:
                            R.op("dve", lambda v, U=U, hh=hh, hl=hl: v.scalar_tensor_tensor(
                                IMP[:, :], U[:, hh, 65:129], SM[:, hl:hl + 1], IMP[:, :], ALU.mult, ALU.add),
                                [("P", bank), "SM", "IMP"], ["IMP"])
                R.op("dve", lambda v, m=m: v.tensor_tensor(IMPF[:, :], IMP[:, :], FBREL[:, 64 - 2 * m:128 - 2 * m], ALU.add),
                     ["IMP", "FBREL"], ["IMPF"])
                R.op("dve", lambda v: v.memset(IMPF[:, 0:1], 4.0e9), ["IMPF"], ["IMPF"])
                R.op("dve", lambda v: v.max(M8[:, 0:8], IMPF[:, :]), ["IMPF"], ["M8"])
                R.op("dve", lambda v: v.match_replace(IMPW[:, :], M8[:, 0:8], IMPF[:, :], -5.0e9), ["IMPF", "M8"], ["IMPW"])
                R.op("dve", lambda v: v.max(M8[:, 8:16], IMPW[:, :]), ["IMPW"], ["M8"])
                R.op("dve", lambda v: v.tensor_reduce(SM[:, 8:9], M8[:, 8:16], AX.X, ALU.min), ["M8"], ["SM"])
                R.op("dve", lambda v, g=g: v.tensor_scalar(
                    MSEL[:, (1 - g) * 64:(1 - g) * 64 + 64], IMPF[:, :], SM[:, 8:9], -NEG, ALU.is_lt, ALU.mult),
                    ["IMPF", "SM"], ["MSEL"])
            pv = PSB(6)
            pe_transposes([(pv[:, 0:128], MSEL[:, :])], ["MSEL"], [("P", 6)])
            R.op("dve", lambda v, pv=pv, qc=qc: v.tensor_copy(
                QM[64:128, 0:4, qc], pv[64:128, 0:128].unsqueeze(1).broadcast_to([64, 4, 128])), [("P", 6)], [("QMm", 0)])
            R.op("act", lambda a, pv=pv, qc=qc: a.activation(
                QM[0:64, 4:8, qc], pv[0:64, 0:128].unsqueeze(1).broadcast_to([64, 4, 128]), AF.Copy), [("P", 6)], [("QMm", 1)])

            def branch(br, g):
                ph = slice(64 * g, 64 * g + 64)
                bank = 4 + g
                if br == 1:
                    kts = list(range(0, m + 1))
                else:
                    kts = list(range(max(0, m - 4), m + 1))
                O = PS[bank][:, 0:260].rearrange("p (h c) -> p h c", c=65)
                for ii, kt in enumerate(kts):
                    b = sbank()
                    ks = slice(kt * 128, (kt + 1) * 128)
                    if br == 1:
                        pe_group([(PS[b][:, :], KE[g][:, ks], QM[:, 4 * g:4 * g + 4, qc], True, True)],
                                 [("KE", g, kt // 4), "KE0x", "KE1x", ("QMm", g)] + qkeys, [("P", b)])
                    else:
                        pe_group([(PS[b][:, :], KW[ph, ks], QM[ph, 4 * g:4 * g + 4, qc], True, True)],
                                 [("KW", kt // 4)] + qkeys, [("P", b)])
                    eb = etbuf()
                    R.op("act", lambda a, b=b, eb=eb: a.activation(ET[:, eb, :], PS[b][:, :], AF.Exp, scale=SCALE),
                         [("P", b)], [("ET", eb)])
                    tris = []
                    if kt == m:
                        tris.append(0)
                    if br == 2 and kt == m - 4:
                        tris.append(1)
                    for tr in tris:
                        R.op("pool", lambda p, eb=eb, tr=tr: p.tensor_tensor(
                            ET[:, eb, :].rearrange("p (h q) -> p h q", q=128),
                            ET[:, eb, :].rearrange("p (h q) -> p h q", q=128),
                            TRI[:, tr, :].unsqueeze(1).broadcast_to([128, 4, 128]), ALU.mult),
                            [("ET", eb), "TRI"], [("ET", eb)])
                    V1 = VS1 if br == 1 else VW1
                    vkey = ("VS1", kt) if br == 1 else ("VW1", kt)
                    okey = "VS1o" if br == 1 else "VW1o"
                    mms = [(O[:, hl, :], ET[:, eb, hl * 128:(hl + 1) * 128], V1[:, kt, g, :], ii == 0 and hl == 0, ii == len(kts) - 1)
                           for hl in range(4)]
                    pe_group(mms, [("ET", eb), vkey, okey], [("P", bank)])
                R.op("dve", lambda v, O=O: v.tensor_scalar(SM[:, 16:20], O[:, :, 64], 1e-30, None, ALU.max), [("P", bank)], ["SM"])
                R.op("dve", lambda v: v.reciprocal(SM[:, 16:20], SM[:, 16:20]), ["SM"], ["SM"])
                R.op("dve", lambda v, g=g, br=br, G3=G3: v.tensor_tensor(SM[:, 20:24], SM[:, 16:20], G3[:, 4 * g:4 * g + 4, br], ALU.mult),
                     ["SM", "GATES"], ["SM"])
                R.op("dve", lambda v, O=O: v.tensor_tensor(
                    OTMP[:, :, :], O[:, :, 0:64], SM[:, 20:24].unsqueeze(2).broadcast_to([128, 4, 64]), ALU.mult),
                    [("P", bank), "SM"], ["OTMP"])
                R.op("pool", lambda p, g=g: p.tensor_tensor(
                    OACC[:, 4 * g:4 * g + 4, :], OACC[:, 4 * g:4 * g + 4, :], OTMP[:, :, :], ALU.add), ["OTMP", "OACC"], ["OACC"])

            for br in (1, 2):
                for g in range(2):
                    branch(br, g)
            R.op("act", lambda a: a.activation(OB[:, :], OACC[:, :, :].rearrange("p h d -> p (h d)"), AF.Copy), ["OACC"], ["OB"])
            pv = PSB(7)
            pe_transposes([(pv[:, c_ * 128:(c_ + 1) * 128], OB[:, c_ * 128:(c_ + 1) * 128]) for c_ in range(4)], ["OB"], [("P", 7)])
            R.op("dve", lambda v, pv=pv, qc=qc: v.tensor_copy(
                MIXT[:, 4:8, qc], pv[:, 0:512].rearrange("p (c q) -> p c q", q=128)), [("P", 7)], ["MIXT"])

        if stop <= 3:
            return
        def proj_resid(srcT, skey):
            slots = take(4)
            for j in range(4):
                for half in range(2):
                    b = nb()
                    mms = []
                    for k in range(8):
                        Wk = WR[:, slots[k // 2], :].rearrange("p (a n) -> p a n", n=1024)
                        mms.append((PS[b][:, :], srcT[:, k, j * 128:(j + 1) * 128], Wk[:, k % 2, half * 512:(half + 1) * 512], k == 0, k == 7))
                    pe_group(mms, [skey] + [("W", s_) for s_ in slots], [("P", b)])
                    R.op("dve", lambda v, b=b, j=j, half=half: v.tensor_tensor(
                        X[:, j, half * 512:(half + 1) * 512], PS[b][:, :], X[:, j, half * 512:(half + 1) * 512], ALU.add),
                        [("P", b), "X"], ["X"])
            release()

        if DBG_OUT:
            R.dma("sp", dbg_mix[:, :, c0:c0 + TT].rearrange("k p t -> p k t"), MIXT[:, :, :], reads=["MIXT"], writes=[("dbgm", st)])
            out_keys.append(("dbgm", st))
        proj_resid(MIXT, "MIXT")
        if DBG_OUT:
            R.dma("sp", dbg_h1[c0:c0 + TT, :].rearrange("(j p) d -> p j d", p=128), X[:, :, :], reads=["X"], writes=[("dbg1", st)])
            out_keys.append(("dbg1", st))

        if stop <= 4:
            return
        norm_T(4, 1, 512)
        slots = take(4)
        for oc in range(8):
            b = nb()
            mms = []
            for k in range(8):
                Wk = WR[:, slots[k // 2], :].rearrange("p (a n) -> p a n", n=1024)
                mms.append((PS[b][:, :], Wk[:, k % 2, oc * 128:(oc + 1) * 128], XNT[:, k, :], k == 0, k == 7))
            pe_group(mms, ["XNT"] + [("W", s_) for s_ in slots], [("P", b)])
            copy_any(QM[:, oc, :], PS[b][:, :], [("P", b)], [("QMq", h) for h in range(8)] + [("QMm", 0), ("QMm", 1)])
        release()
        qall = [("QMq", h) for h in range(8)] + [("QMm", 0), ("QMm", 1)]
        for hx in range(4):
            pbuf = hx % 2
            for mt in range(2):
                b = 2 + mt
                mms = [(PS[b][:, :], KXT[:, hx * 2 + dc, mt * 128:(mt + 1) * 128], QM[:, hx * 2 + dc, :], dc == 0, dc == 1) for dc in range(2)]
                pe_group(mms, ["KXT"] + qall, [("P", b)])
                R.op("act", lambda a, b=b, pbuf=pbuf, mt=mt: a.activation(PXT[:, pbuf, mt, :], PS[b][:, :], AF.Exp, scale=SCALE_X),
                     [("P", b)], [("ET", pbuf * 2 + mt)])
            for j in range(4):
                b = 4 + (j % 2)
                mms = [(PS[b][:, 0:257], PXT[:, pbuf, mt, j * 128:(j + 1) * 128], VX1[:, mt, hx, :], mt == 0, mt == 1) for mt in range(2)]
                pe_group(mms, [("ET", pbuf * 2), ("ET", pbuf * 2 + 1), "VX1", "VX1o"], [("P", b)])
                R.op("dve", lambda v, b=b: v.reciprocal(SM[:, 32:33], PS[b][:, 256:257]), [("P", b)], ["SM"])
                R.op("dve", lambda v, b=b, j=j, hx=hx: v.tensor_scalar(
                    OX[:, j, hx * 256:(hx + 1) * 256], PS[b][:, 0:256], SM[:, 32:33], None, ALU.mult), [("P", b), "SM"], ["BIG"])
        for k2 in range(4):
            bank = 6 + (k2 % 2)
            pv = PSB(bank)
            items = []
            for kk in range(2):
                k = 2 * k2 + kk
                for j in range(4):
                    items.append((pv[:, kk * 512 + j * 128: kk * 512 + (j + 1) * 128], OX[:, j, k * 128:(k + 1) * 128]))
            pe_transposes(items, ["BIG"], [("P", bank)])
            copy_any(MIXT[:, 2 * k2:2 * k2 + 2, :], pv[:, :].rearrange("p (a t) -> p a t", t=512), [("P", bank)], ["MIXT"])
        proj_resid(MIXT, "MIXT")
        if DBG_OUT:
            R.dma("sp", dbg_h2[c0:c0 + TT, :].rearrange("(j p) d -> p j d", p=128), X[:, :, :], reads=["X"], writes=[("dbg2", st)])
            out_keys.append(("dbg2", st))

        if stop <= 5:
            return
        norm_T(4, 3, 512)
        R.op("dve", lambda v: v.memset(SM[:, 40:41], 0.0), (), ["FS"] + FFN_KEYS)
        for c in range(NFC):
            (s,) = take(1)
            W = WR[:, s, :].rearrange("p (k n) -> p k n", n=256)
            ub = c % 2
            for gv in range(2):
                b = nb()
                pe_group([(PS[b][:, :], W[:, k, gv * 128:(gv + 1) * 128], XNT[:, k, :], k == 0, k == 7) for k in range(8)],
                         ["XNT", ("W", s)], [("P", b)])
                R.op("pool", lambda p, ub=ub, gv=gv, c=c: p.tensor_copy(UP[:, ub, gv, 0:2], CARRY[:, c, gv, :]),
                     ["CARRY"], [("UP", ub, gv)])
                R.op("act", lambda a, b=b, ub=ub, gv=gv: a.activation(UP[:, ub, gv, 2:514], PS[b][:, :], AF.Copy),
                     [("P", b)], [("UP", ub, gv)])
                R.op("pool", lambda p, ub=ub, gv=gv, c=c: p.tensor_copy(CARRY[:, c, gv, :], UP[:, ub, gv, 512:514]),
                     [("UP", ub, gv)], ["CARRY"])
                ck = ("CGV", ub, gv)
                R.op("dve", lambda v, ub=ub, gv=gv, c=c: v.tensor_scalar(
                    CGV[:, ub, gv, :], UP[:, ub, gv, 2:514], CW[:, c, gv, 2:3], CB[:, c, gv:gv + 1], ALU.mult, ALU.add),
                    [("UP", ub, gv), "CW", "CB"], [ck])
                for tap in (1, 0):
                    R.op("dve", lambda v, ub=ub, gv=gv, c=c, tap=tap: v.scalar_tensor_tensor(
                        CGV[:, ub, gv, :], UP[:, ub, gv, tap:tap + 512], CW[:, c, gv, tap:tap + 1], CGV[:, ub, gv, :], ALU.mult, ALU.add),
                        [("UP", ub, gv), "CW", ck], [ck])
            R.op("act", lambda a, ub=ub: a.activation(SG[:, ub, :], CGV[:, ub, 0, :], AF.Silu), [("CGV", ub, 0)], [("SG", ub)])
            R.op("pool", lambda p, ub=ub, c=c: p.tensor_tensor(ACTT[:, c, :], SG[:, ub, :], CGV[:, ub, 1, :], ALU.mult),
                 [("SG", ub), ("CGV", ub, 1)], ["BIG"])
            if last:
                b = nb()
                pe_group([(PS[b][0:2, 0:256], XNT[:, k, 510:512], W[:, k, :], k == 0, k == 7) for k in range(8)],
                         ["XNT", ("W", s)], [("P", b)])
                fb_ = c % 2
                R.op("dve", lambda v, b=b, fb_=fb_: v.tensor_copy(FTOK[0:2, fb_, :], PS[b][0:2, 0:256]), [("P", b)], [("FTOK", fb_)])
                R.dma("sp", ffn_p[:, c * 128:(c + 1) * 128], FTOK[0:2, fb_, 0:128], reads=[("FTOK", fb_)], writes=[("o_ffn", c, 0)])
                R.dma("sp", ffn_p[:, DFF + c * 128:DFF + (c + 1) * 128], FTOK[0:2, fb_, 128:256], reads=[("FTOK", fb_)], writes=[("o_ffn", c, 1)])
                out_keys.append(("o_ffn", c, 0))
                out_keys.append(("o_ffn", c, 1))
            release()
        for c2 in range(11):
            (s,) = take(1)
            Wd = WR[:, s, :].rearrange("p (a n) -> p a n", n=1024)
            mms = []
            for kk in range(2):
                for j in range(4):
                    for half in range(2):
                        mms.append((PS[j * 2 + half][:, :], ACTT[:, 2 * c2 + kk, j * 128:(j + 1) * 128],
                                    Wd[:, kk, half * 512:(half + 1) * 512], c2 == 0 and kk == 0, c2 == 10 and kk == 1))
            pe_group(mms, ["BIG", ("W", s)], [("P", i) for i in range(8)])
            release()
        for j in range(4):
            for half in range(2):
                b = j * 2 + half
                R.op("dve", lambda v, b=b, j=j, half=half: v.tensor_tensor(
                    X[:, j, half * 512:(half + 1) * 512], PS[b][:, :], X[:, j, half * 512:(half + 1) * 512], ALU.add),
                    [("P", b), "X"], ["X"])
        for j in range(4):
            R.op("act", lambda a, j=j: a.activation(XS[:, j, :], X[:, j, :], AF.Square, accum_out=SSQ[:, j:j + 1]),
                 ["X"], ["BIG", "SSQ"])
        R.op("dve", lambda v: v.tensor_scalar(RSTD[:, 0:4], SSQ[:, 0:4], 1.0 / D, EPS, ALU.mult, ALU.add), ["SSQ"], ["RSTD"])
        R.op("act", lambda a: a.sqrt(RSTD[:, 0:4], RSTD[:, 0:4]), ["RSTD"], ["RSTD"])
        R.op("dve", lambda v: v.reciprocal(RSTD[:, 0:4], RSTD[:, 0:4]), ["RSTD"], ["RSTD"])
        R.op("dve", lambda v: v.memset(SSQ[:, :], 0.0), ["RSTD"], ["SSQ"])
        for j in range(4):
            if j % 2 == 0:
                R.op("dve", lambda v, j=j: v.scalar_tensor_tensor(X[:, j, :], X[:, j, :], RSTD[:, j:j + 1], GFIN[:, :], ALU.mult, ALU.mult),
                     ["X", "RSTD", "GFIN"], ["X"])
            else:
                R.op("act", lambda a, j=j: a.activation(X[:, j, :], X[:, j, :], AF.Copy, scale=RSTD[:, j:j + 1]), ["X", "RSTD"], ["X"])
                R.op("pool", lambda p, j=j: p.tensor_tensor(X[:, j, :], X[:, j, :], GFIN[:, :], ALU.mult), ["X", "GFIN"], ["X"])
        R.dma("sp", y_p[c0:c0 + TT, :].rearrange("(j p) d -> p j d", p=128), X[:, :, :], reads=["X"], writes=[("o_y", st)])
        out_keys.append(("o_y", st))

    if stop >= 1 and do_prompt:
        for st in range(NST):
            st_body(st)
    if not do_sample:
        return finish()

    R.wait_all("sp", out_keys + [("W", s_) for s_ in range(NSLOT)])
    fin = [(e_, R.cnt[e_]) for e_ in ("pe", "act", "dve", "pool") if R.cnt[e_] > 0]
    R.lists["sp"].append((fin, None, None))
    R.flush()
    pstack.close()
    stacks_to_close.clear()
    del out_keys[:]
    sstack = ExitStack()
    stacks_to_close.append(sstack)

    def sb2(name, shape, dt):
        return sstack.enter_context(nc.sbuf_tensor(name, list(shape), dt))

    XS16 = sb2("XS16", [16, D], F32)
    XB16 = sb2("XB16", [16, D], BF16)
    XNTs = sb2("XNTs", [128, 8, 16], BF16)
    ZT = sb2("ZT", [128, 16, 16], F32)
    QS = sb2("QS", [128, 4, 16], BF16)
    QZ = sb2("QZ", [128, 16, 8], BF16)
    VM = sb2("VM", [128, 8], F32)
    ZTOK = sb2("ZTOK", [16, 2048], F32)
    GS = sb2("GS", [16, 24], F32)
    SPOOL = sb2("SPOOL", [16, 15, 128], F32)
    D16 = sb2("D16", [16, 512], F32)
    D16b = sb2("D16b", [16, 512], BF16)
    DTs = sb2("DTs", [128, 4, 16], BF16)
    MIXTs = sb2("MIXTs", [128, 4, 16], BF16)
    IOTA = sb2("IOTA", [128, 1], I32)
    IDX = sb2("IDX", [128, 1024], I32)
    PG = sb2("PG", [128, 64, 512], BF16)
    XT1 = sb2("XT1", [128, 8192], BF16)
    XT2 = sb2("XT2", [128, 8192], BF16)
    KCs = sb2("KCs", [128, 512], BF16)
    VCTs = sb2("VCTs", [128, 512], BF16)
    VCs = sb2("VCs", [128, 4, 128], BF16)
    OVLS = sb2("OVLS", [128, 4, 128], BF16)
    CVAL = sb2("CVAL", [128, 4], F32)
    FBS = sb2("FBS", [2, 129], F32)
    TMPG = sb2("TMPG", [16, 24], F32)
    IDENTF = sb2("IDENTF", [128, 128], F32)
    ONESB = sb2("ONESB", [128, 128], BF16)
    HL = sb2("HL", [128, 2, 24], BF16)
    TMPQb = sb2("TMPQb", [16, D], BF16)
    SFTb = sb2("SFTb", [16, 2, 2, 128], BF16)
    EC = sb2("EC", [128, 4, 8], BF16)
    ES = sb2("ES", [128, 64, 8], BF16)
    EW = sb2("EW", [128, 4, 8], BF16)
    F8 = sb2("F8", [128, 16, 8], F32)
    IMPs = sb2("IMPs", [2, 129], F32)
    IMPWs = sb2("IMPWs", [2, 129], F32)
    SEL01 = sb2("SEL01", [2, 128], F32)
    SELB = sb2("SELB", [128, 2, 128], F32)
    SELM = sb2("SELM", [128, 64, 2], BF16)
    CWINs = sb2("CWINs", [128, 4, 256], BF16)
    KWTs = sb2("KWTs", [128, 512], BF16)
    GBC = sb2("GBC", [128, 24], F32)
    OFIN = sb2("OFIN", [128, 8, 16], BF16)
    QX16 = sb2("QX16", [16, D], F32)
    WOUTH = XT2[:, 0:4096].rearrange("p (a c) -> p a c", c=1024)
    KVM = PG[:, 0:8, :].rearrange("p n c -> p (n c)").rearrange("p (o t x) -> p o t x", o=1, t=2)
    PROD = sb2("PROD", [128, 1, 1024], F32)
    SX = sb2("SX", [128, 2, 4], F32)
    EXb = sb2("EXb", [128, 2, 4], BF16)
    OXTs = sb2("OXTs", [128, 8, 16], BF16)
    ACTs = sb2("ACTs", [128, NFC, 16], BF16)
    SFT = sb2("SFT", [16, 2, 2, 2, 128], F32)
    FT16 = sb2("FT16", [16, 2, 256], F32)
    CGs = sb2("CGs", [128, 2, 16], F32)
    SGs = sb2("SGs", [128, 16], F32)

    def f8(i):
        return F8[:, i, :]

    R.dma("sp", IDENTF[:, :], c_identf, writes=["IDENTF"])
    R.dma("pool", OVLS[:, :, :], c_ovls, writes=["OVLS"])
    R.dma("sp", CVAL[:, :], c_cval, writes=["CVAL"])
    R.dma("sp", FBS[:, :], c_fbs, writes=["FBS"])
    R.op("pool", lambda p: p.memset(ONESB[:, :], 1.0), (), ["ONESB"])
    R.op("pool", lambda p: p.memset(KCs[:, :], 0.0), (), ["KCs"])
    R.op("pool", lambda p: p.memset(VCTs[:, :], 0.0), (), ["VCTs"])
    R.op("pool", lambda p: p.memset(F8[:, :, :], 0.0), (), ["F8"])
    R.op("pool", lambda p: p.iota(IOTA[:, :], pattern=[[0, 1]], base=0, channel_multiplier=1), (), ["IOTA"])
    R.dma("sp", IDX[:, :], ptab[0:1, :].broadcast_to([128, 1024]), writes=["IDX"])
    R.op("dve", lambda v: v.tensor_scalar(IDX[:, :], IDX[:, :], 128, IOTA[:, 0:1], ALU.mult, ALU.add), ["IDX", "IOTA"], ["IDX"])
    R.dma("sp", XS16[:, :], xs_d, writes=["XS16"])
    R.dma("sp", pool_s[:, 0:14, :], spool[:, 1:15, :], writes=["o_pool_s0"])
    R.dma("sp", win_s[:, 0:511, :], cwin[:, 1:512, :], writes=["o_win_s0"])
    R.dma("sp", ffn_s[:, 0, :], sffn[:, 1, :], writes=["o_ffn_s0"])
    out_keys.extend(["o_pool_s0", "o_win_s0", "o_ffn_s0"])

    if SSTOP <= 1:
        return finish()
    def norm_T16(gsel):
        R.op("act", lambda a: a.activation(XB16[:, :], XS16[:, :], AF.Square, accum_out=SSQ[0:16, 0:1]), ["XS16"], ["XB16", "SSQ"])
        R.op("dve", lambda v: v.tensor_scalar(RSTD[0:16, 0:1], SSQ[0:16, 0:1], 1.0 / D, EPS, ALU.mult, ALU.add), ["SSQ"], ["RSTD"])
        R.op("act", lambda a: a.sqrt(RSTD[0:16, 0:1], RSTD[0:16, 0:1]), ["RSTD"], ["RSTD"])
        R.op("dve", lambda v: v.reciprocal(RSTD[0:16, 0:1], RSTD[0:16, 0:1]), ["RSTD"], ["RSTD"])
        R.op("dve", lambda v: v.memset(SSQ[:, :], 0.0), ["RSTD"], ["SSQ"])
        R.op("act", lambda a: a.activation(XB16[:, :], XS16[:, :], AF.Copy, scale=RSTD[0:16, 0:1]), ["XS16", "RSTD"], ["XB16"])
        pv = PSB(6)
        pe_transposes_id([(pv[:, k * 16:(k + 1) * 16], XB16[0:16, k * 128:(k + 1) * 128]) for k in range(8)], IDENT[0:16, 0:16],
                         ["XB16", "IDENT"], [("P", 6)])
        R.op("dve", lambda v: v.tensor_tensor(XNTs[:, :, :], pv[:, 0:128].rearrange("p (k b) -> p k b", b=16),
                                              GT[:, gsel, :].unsqueeze(2).broadcast_to([128, 8, 16]), ALU.mult),
             [("P", 6), "GT"], ["XNTs"])

    def pe_transposes_id(items, ident, reads, writes):
        def fn(pe, items=items, ident=ident):
            ins = None
            for (o, i_) in items:
                ins = pe.transpose(o, i_, ident)
            return ins
        return R.op("pe", fn, reads, writes)

    pbs = {"n": 0}

    def nbs():
        b = pbs["n"] % 2
        pbs["n"] += 1
        return b

    norm_T16(0)
    for c in range(8):
        (s_,) = take(1)
        W = WR[:, s_, :].rearrange("p (k n) -> p k n", n=256)
        wk = ("W", s_)
        for hh in range(2):
            b = nbs()
            pe_group([(PS[b][:, 0:16], W[:, k, hh * 128:(hh + 1) * 128], XNTs[:, k, :], k == 0, k == 7) for k in range(8)],
                     ["XNTs", wk], [("P", b)])
            R.op("dve", lambda v, b=b, c=c, hh=hh: v.tensor_copy(ZT[:, 2 * c + hh, :], PS[b][:, 0:16]), [("P", b)], ["ZT"])
        b = nbs()
        pe_group([(PS[b][0:16, 0:256], XNTs[:, k, :], W[:, k, :], k == 0, k == 7) for k in range(8)], ["XNTs", wk], [("P", b)])
        R.op("act", lambda a, b=b, c=c: a.activation(ZTOK[0:16, c * 256:(c + 1) * 256], PS[b][0:16, 0:256], AF.Copy), [("P", b)], ["ZTOK"])
        release()
    R.op("dve", lambda v: v.tensor_copy(QS[:, :, :], ZT[:, 4:8, :]), ["ZT"], ["QS"])
    R.op("pool", lambda p: p.memset(QZ[:, :, :], 0.0), (), ["QZ"])
    R.op("pool", lambda p: p.memset(VM[:, :], 0.0), (), ["VM"])
    R.op("pool", lambda p: p.memset(VM[0:64, 0:4], 1.0), ["VM"], ["VM"])
    R.op("pool", lambda p: p.memset(VM[64:128, 4:8], 1.0), ["VM"], ["VM"])
    R.op("dve", lambda v: v.tensor_copy(QZ[0:64, :, 0:4], ZT[0:64, 4:8, :].rearrange("p j b -> p b j")), ["ZT", "QZ"], ["QZ"])
    R.op("dve", lambda v: v.tensor_copy(QZ[64:128, :, 4:8], ZT[64:128, 4:8, :].rearrange("p j b -> p b j")), ["ZT", "QZ"], ["QZ"])
    R.op("act", lambda a: a.activation(GS[:, :], ZTOK[0:16, 1792:1816], AF.Sigmoid), ["ZTOK"], ["GS"])
    R.dma("sp", kv_s, ZTOK[0:16, 1024:1536], reads=["ZTOK"], writes=["o_kv_s"])
    R.dma("sp", win_s[:, 511, :], ZTOK[0:16, 1536:1792], reads=["ZTOK"], writes=["o_win_s1"])
    R.dma("sp", pool_s[:, 14, :], ZTOK[0:16, 0:512], reads=["ZTOK"], writes=["o_pool_s1"])
    out_keys.extend(["o_kv_s", "o_win_s1", "o_pool_s1"])

    if SSTOP <= 2:
        return finish()
    for gi, w in enumerate((2, 4, 8, 16)):
        cs = slice(gi * 128, (gi + 1) * 128)
        R.dma("sp", SPOOL[:, :, :], spool[:, :, cs], writes=["SPOOL"])
        R.op("dve", lambda v, cs=cs, w=w: v.tensor_reduce(D16[0:16, cs], SPOOL[0:16, 16 - w:15, :].rearrange("p r c -> p c r"), AX.X, ALU.add),
             ["SPOOL"], ["D16"])
        R.op("dve", lambda v, cs=cs: v.tensor_tensor(D16[0:16, cs], D16[0:16, cs], ZTOK[0:16, cs], ALU.add), ["D16", "ZTOK"], ["D16"])
        R.op("dve", lambda v, cs=cs, w=w: v.scalar_tensor_tensor(D16b[0:16, cs], D16[0:16, cs], 1.0 / w, ZTOK[0:16, cs], ALU.mult, ALU.subtract),
             ["D16", "ZTOK"], ["D16b"])
    pv = PSB(7)
    pe_transposes_id([(pv[:, gi * 16:(gi + 1) * 16], D16b[0:16, gi * 128:(gi + 1) * 128]) for gi in range(4)], IDENT[0:16, 0:16],
                     ["D16b", "IDENT"], [("P", 7)])
    R.op("dve", lambda v, pv=pv: v.tensor_copy(DTs[:, :, :], pv[:, 0:64].rearrange("p (g b) -> p g b", b=16)), [("P", 7)], ["DTs"])
    for gi in range(4):
        b = nbs()
        pe_group([(PS[b][:, 0:16], WPOOL[:, gi, :], DTs[:, gi, :], True, True)], ["DTs", "WPOOL"], [("P", b)])
        R.op("dve", lambda v, b=b, gi=gi: v.tensor_scalar(MIXTs[:, gi, :], PS[b][:, 0:16], PSCT[:, gi:gi + 1], None, ALU.mult),
             [("P", b), "PSCT"], ["MIXTs"])

    if SSTOP <= 3:
        return finish()
    def replicate_sum(dst_slot, src_ap_f32, bank):
        n = src_ap_f32.shape[1]
        R.op("dve", lambda v: v.tensor_copy(HL[:, 0, 0:n], src_ap_f32), ["F8"], ["HL"])
        R.op("dve", lambda v: v.tensor_tensor(HL[:, 1, 0:n], src_ap_f32, HL[:, 0, 0:n], ALU.subtract), ["F8", "HL"], ["HL"])
        pe_group([(PS[bank][:, 0:n], ONESB[:, :], HL[:, 0, 0:n], True, False), (PS[bank][:, 0:n], ONESB[:, :], HL[:, 1, 0:n], False, True)],
                 ["HL", "ONESB"], [("P", bank)])

    def new_token(bank, kcol, b):
        PRD = f8(5)
        R.op("dve", lambda v: v.tensor_scalar(PRD[0:64, 0:4], ZT[0:64, 4:8, b], ZT[0:64, kcol, b:b + 1], None, ALU.mult), ["ZT", "F8"], ["F8"])
        R.op("dve", lambda v: v.tensor_scalar(PRD[64:128, 4:8], ZT[64:128, 4:8, b], ZT[64:128, kcol, b:b + 1], None, ALU.mult), ["ZT", "F8"], ["F8"])
        replicate_sum(None, PRD, bank)
        R.op("act", lambda a: a.activation(f8(6), PS[bank][:, 0:8], AF.Exp, scale=SCALE), [("P", bank)], ["F8"])

    for b in range(NS):
        if QUIESCE:
            R.op("pool", lambda p: p.memset(SM[0:1, 51:52], 0.0), [("QZ", e_) for e_ in ("pe", "act", "dve")] + ["F8", "OFIN", "ES", "EC", "EW", "XT1", "XT2", "MIXTs", "DTs"], ["QUIESCE"])
        for n in range(64):
            def fn(eng, n=n, b=b):
                return eng.indirect_dma_start(out=PG[:, n, :], out_offset=None, in_=pool_kv[:, :],
                                              in_offset=bass.IndirectOffsetOnAxis(ap=IDX[:, b * 64 + n:b * 64 + n + 1], axis=0))
            R.dma_custom("pool", fn, reads=["IDX"] + (["QUIESCE"] if QUIESCE else []), writes=[("PG", n // 8)])
        if QUIESCE:
            for e_ in ("pe", "act", "dve"):
                R.op(e_, (lambda eng: eng.nop()), [("PG", i_) for i_ in range(8)], [("QZ", e_)])
        if SSTOP <= 4:
            return finish()
        for cols, DST, dk in ((slice(0, 128), XT1, "XT1"), (slice(128, 256), XT2, "XT2")):
            for p8 in range(8):
                bank = 6 + (p8 % 2)
                pv = PSB(bank)
                pe_transposes([(pv[:, i * 128:(i + 1) * 128], PG[:, p8 * 8 + i, cols]) for i in range(8)], [("PG", p8)], [("P", bank)])
                copy_any(DST[:, p8 * 1024:(p8 + 1) * 1024], pv[:, :], [("P", bank)], [dk])
        if SSTOP <= 5:
            return finish()
        b0 = nbs()
        pe_group([(PS[b0][:, 0:511], BDK[:, s2, :], XT1[:, s2:s2 + 16 * 510 + 1:16], s2 == 0, s2 == 31) for s2 in range(32)],
                 ["XT1", "BDK"], [("P", b0)])
        R.op("act", lambda a, b0=b0: a.activation(KCs[:, 0:511], PS[b0][:, 0:511], AF.Identity, bias=BIAS[:, 0:1]), [("P", b0), "BIAS"], ["KCs"])
        b1 = nbs()
        pe_group([(PS[b1][:, 0:511], BDV[:, s2, :], XT2[:, s2:s2 + 16 * 510 + 1:16], s2 == 0, s2 == 31) for s2 in range(32)],
                 ["XT2", "BDV"], [("P", b1)])
        R.op("act", lambda a, b1=b1: a.activation(VCTs[:, 0:511], PS[b1][:, 0:511], AF.Identity, bias=BIAS[:, 1:2]), [("P", b1), "BIAS"], ["VCTs"])
        pv = PSB(6)
        pe_transposes([(pv[:, nt * 128:(nt + 1) * 128], VCTs[:, nt * 128:(nt + 1) * 128]) for nt in range(4)], ["VCTs"], [("P", 6)])
        R.op("dve", lambda v, pv=pv: v.tensor_copy(VCs[:, :, :], pv[:, 0:512].rearrange("p (n c) -> p n c", c=128)), [("P", 6)], ["VCs"])
        if SSTOP <= 6:
            return finish()
        mms = [(PS[2][:, nt * 8:nt * 8 + 8], KCs[:, nt * 128:(nt + 1) * 128], QZ[:, b, :], nt == 0, True) for nt in range(4)]
        pe_group(mms, ["KCs", "QZ"], [("P", 2)])
        R.op("act", lambda a: a.activation(EC[:, :, :], PS[2][:, 0:32].rearrange("p (n h) -> p n h", h=8), AF.Exp, scale=SCALE), [("P", 2)], ["EC"])
        R.op("dve", lambda v: v.tensor_tensor(EC[:, :, :], EC[:, :, :], CVAL[:, :].unsqueeze(2).broadcast_to([128, 4, 8]), ALU.mult),
             ["EC", "CVAL"], ["EC"])
        pe_group([(PS[3][:, 0:8], VCs[:, nt, :], EC[:, nt, :], nt == 0, nt == 3) for nt in range(4)], ["VCs", "EC"], [("P", 3)])
        pe_group([(PS[4][:, 0:8], OVLS[:, nt, :], EC[:, nt, :], nt == 0, nt == 3) for nt in range(4)], ["OVLS", "EC"], [("P", 4)])
        R.op("dve", lambda v: v.tensor_reduce(f8(0), EC[:, :, :].rearrange("p n h -> p h n"), AX.X, ALU.add), ["EC"], ["F8"])
        replicate_sum(None, f8(0), 5)
        R.op("dve", lambda v: v.tensor_scalar(f8(1), PS[5][:, 0:8], 1e-30, None, ALU.max), [("P", 5)], ["F8"])
        R.op("dve", lambda v: v.reciprocal(f8(1), f8(1)), ["F8"], ["F8"])
        R.op("dve", lambda v: v.tensor_tensor(f8(2), PS[3][:, 0:8], f8(1), ALU.mult), [("P", 3), "F8"], ["F8"])
        R.op("dve", lambda v: v.tensor_tensor(f8(3), PS[4][:, 0:8], f8(1), ALU.mult), [("P", 4), "F8"], ["F8"])
        if SSTOP <= 7:
            return finish()
        R.op("dve", lambda v: v.tensor_reduce(f8(4)[:, 0:2], f8(3).rearrange("p (g h) -> p g h", h=4), AX.X, ALU.add), ["F8"], ["F8"])
        R.op("dve", lambda v: v.tensor_copy(HL[:, 0, 0:2], f8(4)[:, 0:2]), ["F8"], ["HL"])
        R.op("dve", lambda v: v.tensor_tensor(HL[:, 1, 0:2], f8(4)[:, 0:2], HL[:, 0, 0:2], ALU.subtract), ["F8", "HL"], ["HL"])
        pe_group([(PS[5][0:2, 0:128], HL[:, 0, 0:2], IDENT[:, :], True, False), (PS[5][0:2, 0:128], HL[:, 1, 0:2], IDENT[:, :], False, True)],
                 ["HL", "IDENT"], [("P", 5)])
        R.op("dve", lambda v: v.tensor_tensor(IMPs[0:2, 0:128], PS[5][0:2, 0:128], FBS[0:2, 0:128], ALU.add), [("P", 5), "FBS"], ["IMPs"])
        R.op("dve", lambda v: v.tensor_copy(IMPs[0:2, 128:129], FBS[0:2, 128:129]), ["FBS", "IMPs"], ["IMPs"])
        R.op("dve", lambda v: v.max(M8[0:2, 0:8], IMPs[0:2, :]), ["IMPs"], ["M8"])
        R.op("dve", lambda v: v.match_replace(IMPWs[0:2, :], M8[0:2, 0:8], IMPs[0:2, :], -5.0e9), ["IMPs", "M8"], ["IMPWs"])
        R.op("dve", lambda v: v.max(M8[0:2, 8:16], IMPWs[0:2, :]), ["IMPWs"], ["M8"])
        R.op("dve", lambda v: v.tensor_reduce(SM[0:2, 8:9], M8[0:2, 8:16], AX.X, ALU.min), ["M8"], ["SM"])
        R.op("dve", lambda v: v.tensor_scalar(SEL01[0:2, :], IMPs[0:2, 0:128], SM[0:2, 8:9], None, ALU.is_ge), ["IMPs", "SM"], ["SEL01"])
        if SSTOP <= 8:
            return finish()
        R.dma("sp", scr_sel[b], SEL01[0:2, :], reads=["SEL01"], writes=[("scr", b)])
        R.dma("sp", SELB[:, :, :].rearrange("p g j -> p (g j)"),
              scr_sel[b:b + 1].rearrange("a g j -> a (g j)").broadcast_to([128, 256]), reads=[("scr", b)], writes=["SELB"])
        for half in range(2):
            hp = slice(64 * half, 64 * half + 64)
            R.op("dve", lambda v, hp=hp, half=half: v.tensor_copy(SELM[hp, :, :], SELB[hp, :, half:128:2].rearrange("p g j -> p j g")),
                 ["SELB"], ["SELM"])
        if SSTOP <= 9:
            return finish()
        for p8 in range(8):
            bank = 6 + (p8 % 2)
            pv = PSB(bank)
            pe_transposes([(pv[:, i * 128:(i + 1) * 128], PG[:, p8 * 8 + i, 256:384]) for i in range(8)], [("PG", p8)], [("P", bank)])
            copy_any(XT1[:, p8 * 1024:(p8 + 1) * 1024], pv[:, :], [("P", bank)], ["XT1"])
        if SSTOP <= 9.2:
            return finish()
        mms = [(PS[2][:, pg_ * 8:pg_ * 8 + 8], XT1[:, pg_ * 128:(pg_ + 1) * 128], QZ[:, b, :], pg_ == 0, True) for pg_ in range(64)]
        pe_group(mms, ["XT1", "QZ"], [("P", 2)])
        if SSTOP <= 9.4:
            return finish()
        R.op("act", lambda a: a.activation(ES[:, :, :], PS[2][:, :].rearrange("p (n h) -> p n h", h=8), AF.Exp, scale=SCALE), [("P", 2)], ["ES"])
        if SSTOP <= 9.5:
            return finish()
        R.op("dve", lambda v: v.tensor_tensor(ES[:, :, :].rearrange("p n (g h) -> p n g h", h=4),
                                              ES[:, :, :].rearrange("p n (g h) -> p n g h", h=4),
                                              SELM[:, :, :].unsqueeze(3).broadcast_to([128, 64, 2, 4]), ALU.mult), ["ES", "SELM"], ["ES"])
        if SSTOP <= 9.6:
            return finish()
        pe_group([(PS[3][:, 0:8], PG[:, pg_, 384:512], ES[:, pg_, :], pg_ == 0, pg_ == 63) for pg_ in range(64)],
                 [("PG", i_) for i_ in range(8)] + ["ES"], [("P", 3)])
        if SSTOP <= 9.7:
            return finish()
        R.op("dve", lambda v: v.tensor_reduce(f8(0), ES[:, :, :].rearrange("p n h -> p h n"), AX.X, ALU.add), ["ES"], ["F8"])
        replicate_sum(None, f8(0), 4)
        if SSTOP <= 9.8:
            return finish()
        new_token(5, 10, b)
        if SSTOP <= 9.9:
            return finish()
        R.op("dve", lambda v: v.scalar_tensor_tensor(f8(7), f8(6), ZT[:, 11, b:b + 1], PS[3][:, 0:8], ALU.mult, ALU.add), ["F8", "ZT", ("P", 3)], ["F8"])
        R.op("dve", lambda v: v.tensor_tensor(f8(8), PS[4][:, 0:8], f8(6), ALU.add), [("P", 4), "F8"], ["F8"])
        R.op("dve", lambda v: v.reciprocal(f8(8), f8(8)), ["F8"], ["F8"])
        R.op("dve", lambda v: v.tensor_tensor(f8(9), f8(7), f8(8), ALU.mult), ["F8"], ["F8"])
        if SSTOP <= 10:
            return finish()
        R.dma("pool", CWINs[:, :, :], cwin[b].rearrange("(t p) c -> p t c", p=128), writes=["CWINs"])
        pv = PSB(6)
        pe_transposes([(pv[:, i * 128:(i + 1) * 128], CWINs[:, i, 0:128]) for i in range(4)], ["CWINs"], [("P", 6)])
        copy_any(KWTs[:, :], pv[:, 0:512], [("P", 6)], ["KWTs"])
        mms = [(PS[2][:, t_ * 8:t_ * 8 + 8], KWTs[:, t_ * 128:(t_ + 1) * 128], QZ[:, b, :], t_ == 0, True) for t_ in range(4)]
        pe_group(mms, ["KWTs", "QZ"], [("P", 2)])
        R.op("act", lambda a: a.activation(EW[:, :, :], PS[2][:, 0:32].rearrange("p (n h) -> p n h", h=8), AF.Exp, scale=SCALE), [("P", 2)], ["EW"])
        pe_group([(PS[3][:, 0:8], CWINs[:, t_, 128:256], EW[:, t_, :], t_ == 0, t_ == 3) for t_ in range(4)], ["CWINs", "EW"], [("P", 3)])
        R.op("dve", lambda v: v.tensor_reduce(f8(0), EW[:, :, :].rearrange("p n h -> p h n"), AX.X, ALU.add), ["EW"], ["F8"])
        replicate_sum(None, f8(0), 4)
        new_token(5, 12, b)
        R.op("dve", lambda v: v.scalar_tensor_tensor(f8(7), f8(6), ZT[:, 13, b:b + 1], PS[3][:, 0:8], ALU.mult, ALU.add), ["F8", "ZT", ("P", 3)], ["F8"])
        R.op("dve", lambda v: v.tensor_tensor(f8(8), PS[4][:, 0:8], f8(6), ALU.add), [("P", 4), "F8"], ["F8"])
        R.op("dve", lambda v: v.reciprocal(f8(8), f8(8)), ["F8"], ["F8"])
        R.op("dve", lambda v: v.tensor_tensor(f8(10), f8(7), f8(8), ALU.mult), ["F8"], ["F8"])
        if SSTOP <= 11:
            return finish()
        R.op("dve", lambda v, b=b: v.tensor_scalar(TMPG[0:16, :], GS[0:16, :], IDENTF[0:16, b:b + 1], None, ALU.mult), ["GS", "IDENTF"], ["TMPG"])
        R.op("dve", lambda v: v.tensor_copy(HL[0:16, 0, :], TMPG[0:16, :]), ["TMPG"], ["HL"])
        R.op("dve", lambda v: v.tensor_tensor(HL[0:16, 1, :], TMPG[0:16, :], HL[0:16, 0, :], ALU.subtract), ["TMPG", "HL"], ["HL"])
        pe_group([(PS[5][:, 0:24], ONESB[0:16, :], HL[0:16, 0, :], True, False), (PS[5][:, 0:24], ONESB[0:16, :], HL[0:16, 1, :], False, True)],
                 ["HL", "ONESB"], [("P", 5)])
        R.op("act", lambda a: a.activation(GBC[:, :], PS[5][:, 0:24], AF.Copy), [("P", 5)], ["GBC"])
        G3s = GBC[:, :].rearrange("p (h r) -> p h r", r=3)
        R.op("dve", lambda v, G3s=G3s: v.tensor_tensor(f8(11), f8(2), G3s[:, :, 0], ALU.mult), ["F8", "GBC"], ["F8"])
        R.op("dve", lambda v, G3s=G3s: v.tensor_tensor(f8(12), f8(9), G3s[:, :, 1], ALU.mult), ["F8", "GBC"], ["F8"])
        R.op("dve", lambda v: v.tensor_tensor(f8(11), f8(11), f8(12), ALU.add), ["F8"], ["F8"])
        R.op("dve", lambda v, G3s=G3s: v.tensor_tensor(f8(12), f8(10), G3s[:, :, 2], ALU.mult), ["F8", "GBC"], ["F8"])
        R.op("dve", lambda v: v.tensor_tensor(f8(11), f8(11), f8(12), ALU.add), ["F8"], ["F8"])
        R.op("dve", lambda v, b=b: v.tensor_tensor(OFIN[:, :, b], f8(11), VM[:, :], ALU.mult), ["F8", "VM"], ["OFIN"])

    if SSTOP <= 12:
        return finish()
    def proj_resid16(lhs_list, wkeys_slots):
        for half in range(2):
            b = nbs()
            mms = []
            n = len(lhs_list)
            for i, (lh, rh) in enumerate(lhs_list):
                mms.append((PS[b][0:16, :], lh, rh(half), i == 0, i == n - 1))
            pe_group(mms, ["MIXTs", "OFIN", "OXTs", "WOUTH"] + [("W", s_) for s_ in wkeys_slots], [("P", b)])
            R.op("dve", lambda v, b=b, half=half: v.tensor_tensor(XS16[0:16, half * 512:(half + 1) * 512], PS[b][0:16, :],
                                                                  XS16[0:16, half * 512:(half + 1) * 512], ALU.add), [("P", b), "XS16"], ["XS16"])

    R.dma("pool", WOUTH[:, :, :].rearrange("p a (b c) -> p (a b) c", c=256), wouth_r.rearrange("p a (b c) -> p (a b) c", c=256),
          writes=["WOUTH", "XT2"])
    slots = take(4)
    lst = []
    for k in range(4):
        Wk = WR[:, slots[k // 2], :].rearrange("p (a n) -> p a n", n=1024)
        lst.append((MIXTs[:, k, :], (lambda half, Wk=Wk, k=k: Wk[:, k % 2, half * 512:(half + 1) * 512])))
    for h in range(8):
        lst.append((OFIN[:, h, :], (lambda half, h=h: WOUTH[:, h % 4, half * 512:(half + 1) * 512])))
    proj_resid16(lst, slots)
    release()

    if SSTOP <= 13:
        return finish()
    if DBG_OUT:
        R.dma("sp", dbg_s[0], XS16[0:16, :], reads=["XS16"], writes=["dbgs0"])
        R.dma("sp", dbg_mx[:, 0:4, :], MIXTs[:, :, :], reads=["MIXTs"], writes=["dbgs2"])
        R.dma("sp", dbg_mx[:, 4:12, :], OFIN[:, :, :], reads=["OFIN"], writes=["dbgs3"])
        out_keys.extend(["dbgs0", "dbgs2", "dbgs3"])
    norm_T16(1)
    slots = take(4)
    for half in range(2):
        b = nbs()
        mms = []
        for k in range(8):
            Wk = WR[:, slots[k // 2], :].rearrange("p (a n) -> p a n", n=1024)
            mms.append((PS[b][0:16, :], XNTs[:, k, :], Wk[:, k % 2, half * 512:(half + 1) * 512], k == 0, k == 7))
        pe_group(mms, ["XNTs"] + [("W", s_) for s_ in slots], [("P", b)])
        R.op("act", lambda a, b=b, half=half: a.activation(QX16[0:16, half * 512:(half + 1) * 512], PS[b][0:16, :], AF.Copy), [("P", b)], ["QX16"])
    release()
    for b in range(NS):
        kb = 0
        for t_ in range(2):
            R.dma("pool", KVM[:, kb, t_, :].rearrange("p (a c) -> p a c", c=256), cmem[b, t_ * 128:(t_ + 1) * 128, :].rearrange("p (a c) -> p a c", c=256),
                  writes=[("KVM", kb, t_)] + [("PG", i_) for i_ in range(8)])
        R.op("dve", lambda v, b=b: v.tensor_scalar(TMPQb[0:16, :], QX16[0:16, :], IDENTF[0:16, b:b + 1], None, ALU.mult), ["QX16", "IDENTF"], ["TMPQ"])
        for half in range(2):
            pe_group([(PS[half][:, :], ONESB[0:16, :], TMPQb[0:16, half * 512:(half + 1) * 512], True, True)], ["TMPQ", "ONESB"], [("P", half)])
        for t_ in range(2):
            for half in range(2):
                R.op("dve", lambda v, t_=t_, half=half, kb=kb: v.tensor_tensor(PROD[:, 0, half * 512:(half + 1) * 512],
                                                                               KVM[:, kb, t_, half * 512:(half + 1) * 512], PS[half][:, :], ALU.mult),
                     [("KVM", kb, t_), ("P", half)], ["PROD"])
            R.op("dve", lambda v, t_=t_: v.tensor_reduce(SX[:, t_, :], PROD[:, 0, :].rearrange("p (h d) -> p h d", d=256), AX.X, ALU.add),
                 ["PROD"], ["SX"])
        R.op("act", lambda a: a.activation(EXb[:, :, :], SX[:, :, :], AF.Exp, scale=SCALE_X), ["SX"], ["EXb"])
        mms = []
        for oc in range(8):
            for t_ in range(2):
                mms.append((PS[2][:, oc * 4:oc * 4 + 4], KVM[:, kb, t_, 1024 + oc * 128:1024 + (oc + 1) * 128], EXb[:, t_, :], oc == 0 and t_ == 0, t_ == 1))
        pe_group(mms, [("KVM", kb, 0), ("KVM", kb, 1), "EXb"], [("P", 2)])
        R.op("dve", lambda v: v.tensor_tensor(f8(0)[:, 0:4], EXb[:, 0, :], EXb[:, 1, :], ALU.add), ["EXb"], ["F8"])
        replicate_sum(None, f8(0)[:, 0:4], 3)
        R.op("dve", lambda v: v.reciprocal(f8(1)[:, 0:4], PS[3][:, 0:4]), [("P", 3)], ["F8"])
        for hx in range(4):
            R.op("dve", lambda v, hx=hx, b=b: v.tensor_scalar(OXTs[:, 2 * hx:2 * hx + 2, b],
                                                              PS[2][:, 0:32].rearrange("p (o h) -> p o h", h=4)[:, 2 * hx:2 * hx + 2, hx],
                                                              f8(1)[:, hx:hx + 1], None, ALU.mult), [("P", 2), "F8"], ["OXTs"])
    slots = take(4)
    lst = []
    for k in range(8):
        Wk = WR[:, slots[k // 2], :].rearrange("p (a n) -> p a n", n=1024)
        lst.append((OXTs[:, k, :], (lambda half, Wk=Wk, k=k: Wk[:, k % 2, half * 512:(half + 1) * 512])))
    proj_resid16(lst, slots)
    release()

    if SSTOP <= 14:
        return finish()
    if DBG_OUT:
        R.dma("sp", dbg_s[1], XS16[0:16, :], reads=["XS16"], writes=["dbgs1"])
        out_keys.append("dbgs1")
    norm_T16(3)
    for c in range(NFC):
        (s_,) = take(1)
        W = WR[:, s_, :].rearrange("p (k n) -> p k n", n=256)
        wk = ("W", s_)
        fbuf = c % 2
        b = nbs()
        mms = []
        for gv in range(2):
            for k in range(8):
                mms.append((PS[b][:, gv * 16:(gv + 1) * 16], W[:, k, gv * 128:(gv + 1) * 128], XNTs[:, k, :], k == 0, k == 7))
        pe_group(mms, ["XNTs", wk], [("P", b)])
        b2 = 2 + (c % 2)
        pe_group([(PS[b2][0:16, 0:256], XNTs[:, k, :], W[:, k, :], k == 0, k == 7) for k in range(8)], ["XNTs", wk], [("P", b2)])
        R.op("act", lambda a, b2=b2, fbuf=fbuf: a.activation(FT16[0:16, fbuf, :], PS[b2][0:16, 0:256], AF.Copy), [("P", b2)], [("FT16", fbuf)])
        R.dma("sp", ffn_s[:, 1, c * 128:(c + 1) * 128], FT16[0:16, fbuf, 0:128], reads=[("FT16", fbuf)], writes=[("o_ffs", c, 0)])
        R.dma("sp", ffn_s[:, 1, DFF + c * 128:DFF + (c + 1) * 128], FT16[0:16, fbuf, 128:256], reads=[("FT16", fbuf)], writes=[("o_ffs", c, 1)])
        out_keys.extend([("o_ffs", c, 0), ("o_ffs", c, 1)])
        for gv in range(2):
            R.dma("sp", SFT[0:16, fbuf, gv, :, :], sffn[:, :, gv * DFF + c * 128:gv * DFF + (c + 1) * 128], writes=[("SFT", fbuf)])
        b3 = 4 + (c % 2)
        R.op("dve", lambda v, fbuf=fbuf: v.tensor_copy(SFTb[0:16, :, :, :], SFT[0:16, fbuf, :, :, :]), [("SFT", fbuf)], ["SFTb"])
        pe_transposes_id([(PSB(b3)[:, (gv * 2 + r_) * 16:(gv * 2 + r_ + 1) * 16], SFTb[0:16, gv, r_, :]) for gv in range(2) for r_ in range(2)],
                         IDENT[0:16, 0:16], ["SFTb", "IDENT"], [("P", b3)])
        for gv in range(2):
            R.op("dve", lambda v, b=b, gv=gv, c=c: v.tensor_scalar(CGs[:, gv, :], PS[b][:, gv * 16:(gv + 1) * 16], CW[:, c, gv, 2:3],
                                                                   CB[:, c, gv:gv + 1], ALU.mult, ALU.add), [("P", b), "CW", "CB"], ["CGs"])
            for r_ in (1, 0):
                R.op("dve", lambda v, b3=b3, gv=gv, c=c, r_=r_: v.scalar_tensor_tensor(
                    CGs[:, gv, :], PSB(b3)[:, (gv * 2 + r_) * 16:(gv * 2 + r_ + 1) * 16], CW[:, c, gv, r_:r_ + 1], CGs[:, gv, :], ALU.mult, ALU.add),
                    [("P", b3), "CW", "CGs"], ["CGs"])
        R.op("act", lambda a: a.activation(SGs[:, :], CGs[:, 0, :], AF.Silu), ["CGs"], ["SGs"])
        R.op("dve", lambda v, c=c: v.tensor_tensor(ACTs[:, c, :], SGs[:, :], CGs[:, 1, :], ALU.mult), ["SGs", "CGs"], ["ACTs"])
        release()
    for c2 in range(11):
        (s_,) = take(1)
        Wd = WR[:, s_, :].rearrange("p (a n) -> p a n", n=1024)
        mms = []
        for kk in range(2):
            for half in range(2):
                mms.append((PS[6 + half][0:16, :], ACTs[:, 2 * c2 + kk, :], Wd[:, kk, half * 512:(half + 1) * 512],
                            c2 == 0 and kk == 0, c2 == 10 and kk == 1))
        pe_group(mms, ["ACTs", ("W", s_)], [("P", 6), ("P", 7)])
        release()
    for half in range(2):
        R.op("dve", lambda v, half=half: v.tensor_tensor(XS16[0:16, half * 512:(half + 1) * 512], PS[6 + half][0:16, :],
                                                         XS16[0:16, half * 512:(half + 1) * 512], ALU.add), [("P", 6 + half), "XS16"], ["XS16"])
    R.op("act", lambda a: a.activation(XB16[:, :], XS16[:, :], AF.Square, accum_out=SSQ[0:16, 0:1]), ["XS16"], ["XB16", "SSQ"])
    R.op("dve", lambda v: v.tensor_scalar(RSTD[0:16, 0:1], SSQ[0:16, 0:1], 1.0 / D, EPS, ALU.mult, ALU.add), ["SSQ"], ["RSTD"])
    R.op("act", lambda a: a.sqrt(RSTD[0:16, 0:1], RSTD[0:16, 0:1]), ["RSTD"], ["RSTD"])
    R.op("dve", lambda v: v.reciprocal(RSTD[0:16, 0:1], RSTD[0:16, 0:1]), ["RSTD"], ["RSTD"])
    R.op("dve", lambda v: v.scalar_tensor_tensor(XS16[0:16, :], XS16[0:16, :], RSTD[0:16, 0:1], GFIN[0:16, :], ALU.mult, ALU.mult),
         ["XS16", "RSTD", "GFIN"], ["XS16"])
    R.dma("sp", y_s, XS16[0:16, :], reads=["XS16"], writes=["o_y_s"])
    out_keys.append("o_y_s")
    return finish()


def _consts(T):
    r = np.arange(128)[:, None]
    c = np.arange(128)[None, :]
    ident = (r == c).astype(np.float32)
    tri = np.stack([(r <= c), (r >= c)], axis=1).astype(np.float32)
    dl = np.arange(16)[None, :, None]
    cmask = ((16 * r[:, :, None] + 31 - c[:, None, :]) <= 128 * dl).astype(np.float32)
    q = np.arange(128)[:, None]
    jr = np.arange(130)[None, :] - 64
    jqr = (q >= 64).astype(np.int64)
    fb = np.where(jr > jqr, -1.0e9, 0.0).astype(np.float32)
    fb = np.where(jr == jqr, 3.0e9, fb)
    fb = np.where(jr == jqr - 1, 2.0e9, fb).astype(np.float32)
    n = np.arange(256)[:, None]
    j = np.arange(64)[None, :]
    ov = ((16 * n + 31 >= 64 * j) & (16 * n <= 64 * j + 63)).astype(np.float32)
    ov[255] = 0
    ovl = ov.reshape(2, 128, 64).transpose(1, 0, 2).copy()
    key = np.arange(T)[None, :]
    jj = (np.arange(128) % 64)[:, None]
    expand = ((key // 64) == jj).astype(np.float32)
    t = np.arange(16)[None, None, :]
    w = np.array([2, 4, 8, 16])[None, :, None]
    invc = (1.0 / np.minimum(t + 1, w)).astype(np.float32) * np.ones((128, 1, 1), np.float32)
    ns = np.arange(512)[:, None]
    js = np.arange(128)[None, :]
    ovs = ((16 * ns + 31 >= 64 * js) & (16 * ns <= 64 * js + 63)).astype(np.float32)
    ovs[511] = 0
    ovls = ovs.reshape(4, 128, 128).transpose(1, 0, 2).copy()
    cval = np.ones((128, 4), np.float32)
    cval[127, 3] = 0
    fbs = np.zeros((2, 129), np.float32)
    fbs[:, 0] = 1.0e9
    fbs[:, 127] = 2.0e9
    fbs[:, 128] = 3.0e9
    sel = np.zeros((16, 16, 128), np.float32)
    for b_ in range(16):
        sel[b_, b_, :] = 1.0
    extra = dict(c_identf=ident.copy(), c_ovls=ovls, c_cval=cval, c_fbs=fbs, c_sel=sel)
    return dict(extra, c_ident=ident, c_tri=tri, c_cmask=cmask, c_fbrel=fb, c_ovl=ovl, c_expand=expand, c_invc=invc.astype(np.float32))


def _prep_weights(inp):
    f = np.float32
    w_in = inp["w_in"][0]
    qcols = []
    for jq in range(4):
        qcols += list(range(512 + 64 * jq, 512 + 64 * jq + 64))
        qcols += list(range(512 + 64 * (4 + jq), 512 + 64 * (4 + jq) + 64))
    perm = list(range(512)) + qcols + list(range(1024, 1816))
    wp = np.zeros((D, 2048), f)
    wp[:, :1816] = w_in[:, perm]
    win_r = wp.reshape(8, 128, 8, 256).transpose(2, 1, 0, 3).copy()

    def kchunks(w, nch):
        return w.reshape(nch, 2, 128, w.shape[1]).transpose(0, 2, 1, 3).copy().reshape(nch, 128, 8, 256)

    wout_r = kchunks(inp["w_out"][0], 4)
    wxq_r = kchunks(inp["w_xq"][0], 4)
    wxo_r = kchunks(inp["w_xo"][0], 4)
    w_up = inp["w_up"][0]
    wu = np.concatenate([w_up[:, :DFF].reshape(D, NFC, 1, 128), w_up[:, DFF:].reshape(D, NFC, 1, 128)], axis=2)
    wup_r = wu.reshape(8, 128, NFC, 256).transpose(2, 1, 0, 3).copy()
    wdn_r = kchunks(inp["w_down"][0], 11)
    wxkv_r = inp["w_xkv"][0].reshape(8, 128, 8, 256).transpose(2, 1, 0, 3).copy()
    wpool_r = inp["w_pool"][0].transpose(1, 0, 2).copy()
    w_cmp = inp["w_cmp"][0]
    bd = np.zeros((2, 2, 64, 32, 2, 64), f)
    for g in range(2):
        bd[:, g, :, :, g, :] = w_cmp.transpose(0, 2, 1, 3)
    bdk_r = bd[0].reshape(128, 32, 128).copy()
    bdv_r = bd[1].reshape(128, 32, 128).copy()
    pe = inp["pe_cmp"][0]
    peT = np.broadcast_to(pe.transpose(2, 0, 1)[None], (2, 64, 2, 32)).reshape(128, 2, 32).copy()
    gT = np.stack([inp["g_mix"][0], inp["g_xattn"][0], inp["g_mem"][0], inp["g_ffn"][0]], 0)
    gT_r = gT.reshape(4, 8, 128).transpose(2, 0, 1).copy()
    gfin_r = inp["g_final"].reshape(1, D).copy()
    pscT_r = inp["pool_scale"][0].reshape(4, 128).T.copy()
    cw = inp["conv_w"][0]
    cwT_r = cw.reshape(3, 2, NFC, 128).transpose(3, 2, 1, 0).copy()
    cbT_r = inp["conv_b"][0].reshape(2, NFC, 128).transpose(2, 1, 0).copy()
    wall = np.concatenate([wxkv_r, win_r, wout_r, wxq_r, wxo_r, wup_r, wdn_r], axis=0)
    wouth_r = inp["w_out"][0][512:].reshape(2, 4, 64, D).transpose(0, 2, 1, 3).reshape(128, 4, D).copy()
    return dict(wall=wall, wouth_r=wouth_r,
                wpool_r=wpool_r, bdk_r=bdk_r, bdv_r=bdv_r, peT_r=peT, gT_r=gT_r, gfin_r=gfin_r, pscT_r=pscT_r,
                cwT_r=cwT_r, cbT_r=cbT_r)


_CACHE = {}


def run_cores(inp, cores, T=4096, NS=16, stop=99, do_prompt=True, do_sample=True, pool_override=None):
    nphys = inp["cache_kv"].shape[1] if pool_override is None else pool_override[0].shape[0] // 128
    keyc = (T, NS, stop, do_prompt, do_sample, nphys)
    if keyc not in _CACHE:
        _CACHE[keyc] = build_program(T, NS, stop=stop, do_prompt=do_prompt, do_sample=do_sample, nphys=nphys)
    nc = _CACHE[keyc]
    shared = dict(_prep_weights(inp))
    shared.update(_consts(T))
    if pool_override is None:
        shared["pool_kv"] = inp["cache_kv"][0].reshape(nphys * 128, 512)
    in_maps = []
    for c in cores:
        m = dict(shared)
        m["xp"] = np.ascontiguousarray(inp["x_prompt"][c, :T])
        m["memp"] = np.ascontiguousarray(inp["mem_prompt"][c])
        bs = slice(16 * c, 16 * c + 16)
        m["xs"] = np.ascontiguousarray(inp["x_sample"][bs, 0])
        if pool_override is None:
            m["ptab"] = np.ascontiguousarray(inp["page_table"][bs]).reshape(1, 1024).astype(np.int32)
        else:
            m["pool_kv"] = pool_override[0]
            m["ptab"] = pool_override[1]
        m["cwin"] = np.ascontiguousarray(inp["cache_win"][0, bs]).reshape(16, 512, 256)
        m["spool"] = np.ascontiguousarray(inp["state_pool"][0, bs])
        m["sffn"] = np.ascontiguousarray(inp["state_ffn"][0, bs])
        m["cmem"] = np.ascontiguousarray(inp["cache_mem"][0, bs]).reshape(16, 256, 2048)
        in_maps.append(m)
    res = run_bass_kernel_spmd(nc, in_maps, core_ids=list(range(len(cores))))
    return res.results


def kernel(**inp):
    inp = {k: np.asarray(v) for k, v in inp.items()}
    res = run_cores(inp, list(range(8)), T=4096, NS=16)
    f = np.float32
    y_prompt = np.stack([r["y_p"] for r in res]).astype(f)
    y_sample = np.concatenate([r["y_s"] for r in res]).reshape(128, 1, D).astype(f)
    kv_prompt = np.stack([r["kv_p"] for r in res]).reshape(1, 8, 4096, 4, 2, 64).astype(f)
    kv_sample = np.concatenate([r["kv_s"] for r in res]).reshape(1, 128, 1, 4, 2, 64).astype(f)
    win_prompt = np.stack([r["win_p"] for r in res]).reshape(1, 8, 512, 2, 2, 64).astype(f)
    win_sample = np.concatenate([r["win_s"] for r in res]).reshape(1, 128, 512, 2, 2, 64).astype(f)
    pool_prompt = np.stack([r["pool_p"] for r in res]).reshape(1, 8, 15, 512).astype(f)
    pool_sample = np.concatenate([r["pool_s"] for r in res]).reshape(1, 128, 15, 512).astype(f)
    ffn_prompt = np.stack([r["ffn_p"] for r in res]).reshape(1, 8, 2, 2 * DFF).astype(f)
    ffn_sample = np.concatenate([r["ffn_s"] for r in res]).reshape(1, 128, 2, 2 * DFF).astype(f)
    mem_prompt_kv = np.stack([r["memkv_p"] for r in res]).reshape(1, 8, 256, 2, 4, 256).astype(f)
    return (y_prompt, y_sample, kv_prompt, kv_sample, win_prompt, win_sample,
            pool_prompt, pool_sample, ffn_prompt, ffn_sample, mem_prompt_kv)
```
